# Optimizing a Trainium2 kernel written in Bass

```python
import jax, jax.numpy as jnp
from jax import lax
import numpy as np

D_MODEL = 1024
BATCH = 8
SEQ = 2048
DEPTH = 4

N_META = 16
EPS = 1e-6
RNN_WIDTH = D_MODEL
RNN_BLOCKS = 8
RNN_BLOCK = RNN_WIDTH // RNN_BLOCKS
CONV_WIDTH = 4
RGLRU_C = 8.0
GLA_HEADS = 4
GLA_DK = D_MODEL // 2 // GLA_HEADS
GLA_DV = D_MODEL // GLA_HEADS
GLA_GATE_RANK = 16
GLA_TAU = 16.0
GLA_CHUNK = 64
MLA_HEADS = 16
MLA_NOPE = 64
MLA_ROPE = 32
MLA_V = 64
MLA_Q_RANK = D_MODEL // 2
MLA_KV_RANK = D_MODEL // 4
ROPE_BASE = 10000.0
Q_BLOCK = 128

N_AB = (DEPTH + 1) // 2
N_C = DEPTH // 2

EVEN_SPLITS = (RNN_WIDTH, RNN_WIDTH, GLA_HEADS * GLA_DK, GLA_HEADS * GLA_DK,
               GLA_HEADS * GLA_DV, GLA_GATE_RANK, GLA_HEADS * GLA_DV)
EVEN_IN = sum(EVEN_SPLITS)
EVEN_MIX = RNN_WIDTH + GLA_HEADS * GLA_DV
ODD_SPLITS = (MLA_Q_RANK, MLA_KV_RANK, MLA_ROPE, MLA_HEADS * MLA_V)
ODD_IN = sum(ODD_SPLITS)
ODD_MIX = MLA_HEADS * MLA_V

kernel_name = "hybrid_rglru_gla_mla_meta"


def _split(t, sizes):
    idx = np.cumsum(sizes)[:-1].tolist()
    return jnp.split(t, idx, axis=-1)


def rmsnorm(x, g):
    xf = x.astype(jnp.float32)
    y = xf * lax.rsqrt(jnp.mean(xf * xf, axis=-1, keepdims=True) + EPS)
    return (y * g.astype(jnp.float32)).astype(x.dtype)


def apply_rope(x, cos, sin):
    xf = x.astype(jnp.float32)
    x1, x2 = jnp.split(xf, 2, axis=-1)
    return jnp.concatenate([x1 * cos - x2 * sin, x2 * cos + x1 * sin], axis=-1).astype(x.dtype)


def causal_conv(x, w, b):
    T = x.shape[1]
    xp = jnp.pad(x, ((0, 0), (CONV_WIDTH - 1, 0), (0, 0)))
    y = xp[:, 0:T] * w[0]
    for k in range(1, CONV_WIDTH):
        y = y + xp[:, k:k + T] * w[k]
    return y + b


def rglru(x, gate_a_w, gate_a_b, gate_x_w, gate_x_b, lam):
    B, T, _ = x.shape
    xf = x.astype(jnp.float32)
    xb = xf.reshape(B, T, RNN_BLOCKS, RNN_BLOCK)
    r = jax.nn.sigmoid(jnp.einsum('btgi,gij->btgj', xb, gate_a_w.astype(jnp.float32)).reshape(B, T, RNN_WIDTH)
                       + gate_a_b.astype(jnp.float32))
    i = jax.nn.sigmoid(jnp.einsum('btgi,gij->btgj', xb, gate_x_w.astype(jnp.float32)).reshape(B, T, RNN_WIDTH)
                       + gate_x_b.astype(jnp.float32))
    log_a = -RGLRU_C * r * jax.nn.softplus(-lam.astype(jnp.float32))
    a = jnp.exp(log_a)
    u = jnp.sqrt(-jnp.expm1(2.0 * log_a)) * (i * xf)

    def combine(e1, e2):
        a1, b1 = e1
        a2, b2 = e2
        return a1 * a2, a2 * b1 + b2

    _, h = lax.associative_scan(combine, (a, u), axis=1)
    return h.astype(x.dtype)


def gla_chunked(q, k, v, log_alpha):
    B, T, H, DK = q.shape
    DV = v.shape[-1]
    dtype = v.dtype
    pad = (-N_META) % GLA_CHUNK
    padf = lambda t: jnp.pad(t.astype(jnp.float32), ((0, 0), (pad, 0), (0, 0), (0, 0)))
    q, k, v, la = padf(q), padf(k), padf(v), padf(log_alpha)
    Tp = T + pad
    N = Tp // GLA_CHUNK

    def chunks(t):
        return t.reshape(B, N, GLA_CHUNK, H, t.shape[-1]).transpose(1, 0, 3, 2, 4)

    q, k, v, la = chunks(q), chunks(k), chunks(v), chunks(la)
    b = jnp.cumsum(la, axis=3)
    b_last = b[:, :, :, -1:, :]
    q_dec = q * jnp.exp(b)
    k_inv = k * jnp.exp(-b)
    k_end = k * jnp.exp(b_last - b)
    causal = jnp.tril(jnp.ones((GLA_CHUNK, GLA_CHUNK), dtype=bool))
    s = jnp.where(causal, jnp.einsum('nbhcd,nbhsd->nbhcs', q_dec, k_inv), 0.0)
    o_intra = jnp.einsum('nbhcs,nbhse->nbhce', s, v)

    def step(S, inp):
        qd, ke, vv, bl = inp
        o = jnp.einsum('bhcd,bhde->bhce', qd, S)
        S = jnp.exp(bl)[:, :, 0, :, None] * S + jnp.einsum('bhcd,bhce->bhde', ke, vv)
        return S, o

    S0 = jnp.zeros((B, H, DK, DV), jnp.float32)
    _, o_inter = lax.scan(step, S0, (q_dec, k_end, v, b_last))
    o = (o_intra + o_inter).transpose(1, 0, 3, 2, 4).reshape(B, Tp, H, DV)[:, pad:]
    return o.astype(dtype)


def ab_mixer(h, w_in, conv_w, conv_b, gate_a_w, gate_a_b, gate_x_w, gate_x_b, lam,
             alpha_w, alpha_b, gla_norm, w_out):
    B, T, _ = h.shape
    xa, ga, q, k, v, ad, gb = _split(h @ w_in, EVEN_SPLITS)
    ya = rglru(causal_conv(xa, conv_w, conv_b), gate_a_w, gate_a_b, gate_x_w, gate_x_b, lam)
    ya = ya * jax.nn.silu(ga)
    q = q.reshape(B, T, GLA_HEADS, GLA_DK) * (GLA_DK ** -0.5)
    k = k.reshape(B, T, GLA_HEADS, GLA_DK)
    v = v.reshape(B, T, GLA_HEADS, GLA_DV)
    log_alpha = jax.nn.log_sigmoid((ad @ alpha_w + alpha_b).astype(jnp.float32)) / GLA_TAU
    log_alpha = log_alpha.reshape(B, T, GLA_HEADS, GLA_DK)
    ob = rmsnorm(gla_chunked(q, k, v, log_alpha), gla_norm).reshape(B, T, GLA_HEADS * GLA_DV)
    ob = ob * jax.nn.silu(gb)
    return jnp.concatenate([ya, ob], axis=-1) @ w_out


def causal_mla_attention(q_nope, q_rope, k_nope, k_rope, v):
    T = q_nope.shape[1]
    Tp = -(-T // Q_BLOCK) * Q_BLOCK
    padt = lambda t: jnp.pad(t, ((0, 0), (0, Tp - T)) + ((0, 0),) * (t.ndim - 2))
    q_nope, q_rope, k_nope, k_rope, v = map(padt, (q_nope, q_rope, k_nope, k_rope, v))
    scale = (MLA_NOPE + MLA_ROPE) ** -0.5
    outs = []
    for n in range(Tp // Q_BLOCK):
        q0, q1 = n * Q_BLOCK, (n + 1) * Q_BLOCK
        s = (jnp.einsum('bqhd,bkhd->bhqk', q_nope[:, q0:q1], k_nope[:, :q1])
             + jnp.einsum('bqhr,bkr->bhqk', q_rope[:, q0:q1], k_rope[:, :q1]))
        s = s.astype(jnp.float32) * scale
        qi = jnp.arange(q0, q1)[:, None]
        ki = jnp.arange(q1)[None, :]
        s = jnp.where(ki <= qi, s, -jnp.inf)
        p = jax.nn.softmax(s, axis=-1).astype(v.dtype)
        outs.append(jnp.einsum('bhqk,bkhd->bqhd', p, v[:, :q1]))
    return jnp.concatenate(outs, axis=1)[:, :T]


def mla_mixer(h, w_in, q_norm, w_q_up, kv_norm, w_kv_up, w_out, cos, sin):
    B, T, _ = h.shape
    cq, ckv, k_rope, gate = _split(h @ w_in, ODD_SPLITS)
    q = (rmsnorm(cq, q_norm) @ w_q_up).reshape(B, T, MLA_HEADS, MLA_NOPE + MLA_ROPE)
    q_nope, q_rope = q[..., :MLA_NOPE], q[..., MLA_NOPE:]
    kv = (rmsnorm(ckv, kv_norm) @ w_kv_up).reshape(B, T, MLA_HEADS, MLA_NOPE + MLA_V)
    k_nope, v = kv[..., :MLA_NOPE], kv[..., MLA_NOPE:]
    q_rope = apply_rope(q_rope, cos[:, :, None, :], sin[:, :, None, :])
    k_rope = apply_rope(k_rope, cos, sin)
    o = causal_mla_attention(q_nope, q_rope, k_nope, k_rope, v).reshape(B, T, ODD_MIX)
    return (o * jax.nn.silu(gate)) @ w_out


def setup_inputs(seed: int = 0) -> dict:
    key = jax.random.key(seed)
    ks = iter(jax.random.split(key, 32))
    nrm = lambda shape, scale: jax.random.normal(next(ks), shape, jnp.float32) * scale
    u = jax.random.uniform(next(ks), (N_AB, RNN_WIDTH), jnp.float32, 0.9, 0.999)
    s = u ** (1.0 / RGLRU_C)
    lam = jnp.log(s) - jnp.log1p(-s)
    return {
        "x": nrm((BATCH, SEQ, D_MODEL), 1.0),
        "positions": jnp.broadcast_to(jnp.arange(SEQ, dtype=jnp.int32)[None], (BATCH, SEQ)),
        "meta_tokens": nrm((N_META, D_MODEL), 1.0),
        "ab_norm": 1.0 + nrm((N_AB, D_MODEL), 0.1),
        "ab_w_in": nrm((N_AB, D_MODEL, EVEN_IN), D_MODEL ** -0.5),
        "ab_conv_w": nrm((N_AB, CONV_WIDTH, RNN_WIDTH), CONV_WIDTH ** -0.5),
        "ab_conv_b": nrm((N_AB, RNN_WIDTH), 0.01),
        "ab_gate_a_w": nrm((N_AB, RNN_BLOCKS, RNN_BLOCK, RNN_BLOCK), RNN_BLOCK ** -0.5),
        "ab_gate_a_b": nrm((N_AB, RNN_WIDTH), 0.01),
        "ab_gate_x_w": nrm((N_AB, RNN_BLOCKS, RNN_BLOCK, RNN_BLOCK), RNN_BLOCK ** -0.5),
        "ab_gate_x_b": nrm((N_AB, RNN_WIDTH), 0.01),
        "ab_lru_lambda": lam,
        "ab_alpha_w": nrm((N_AB, GLA_GATE_RANK, GLA_HEADS * GLA_DK), GLA_GATE_RANK ** -0.5),
        "ab_alpha_b": nrm((N_AB, GLA_HEADS * GLA_DK), 0.1),
        "ab_gla_norm": 1.0 + nrm((N_AB, GLA_DV), 0.1),
        "ab_w_out": nrm((N_AB, EVEN_MIX, D_MODEL), EVEN_MIX ** -0.5),
        "c_norm": 1.0 + nrm((N_C, D_MODEL), 0.1),
        "c_w_in": nrm((N_C, D_MODEL, ODD_IN), D_MODEL ** -0.5),
        "c_q_norm": 1.0 + nrm((N_C, MLA_Q_RANK), 0.1),
        "c_w_q_up": nrm((N_C, MLA_Q_RANK, MLA_HEADS * (MLA_NOPE + MLA_ROPE)), MLA_Q_RANK ** -0.5),
        "c_kv_norm": 1.0 + nrm((N_C, MLA_KV_RANK), 0.1),
        "c_w_kv_up": nrm((N_C, MLA_KV_RANK, MLA_HEADS * (MLA_NOPE + MLA_V)), MLA_KV_RANK ** -0.5),
        "c_w_out": nrm((N_C, ODD_MIX, D_MODEL), ODD_MIX ** -0.5),
        "final_norm": 1.0 + nrm((D_MODEL,), 0.1),
    }


def reference(x, positions, meta_tokens, ab_norm, ab_w_in, ab_conv_w, ab_conv_b, ab_gate_a_w,
              ab_gate_a_b, ab_gate_x_w, ab_gate_x_b, ab_lru_lambda, ab_alpha_w, ab_alpha_b,
              ab_gla_norm, ab_w_out, c_norm, c_w_in, c_q_norm, c_w_q_up, c_kv_norm, c_w_kv_up,
              c_w_out, final_norm):
    B = x.shape[0]
    meta = jnp.broadcast_to(meta_tokens.astype(x.dtype)[None], (B, N_META, D_MODEL))
    h = jnp.concatenate([meta, x], axis=1)
    meta_pos = jnp.broadcast_to(jnp.arange(N_META, dtype=positions.dtype)[None], (B, N_META))
    pos = jnp.concatenate([meta_pos, positions + N_META], axis=1)
    inv_freq = ROPE_BASE ** (-jnp.arange(0, MLA_ROPE, 2, dtype=jnp.float32) / MLA_ROPE)
    ang = pos.astype(jnp.float32)[..., None] * inv_freq
    cos, sin = jnp.cos(ang), jnp.sin(ang)
    for layer in range(DEPTH):
        j = layer // 2
        if layer % 2 == 0:
            h = h + ab_mixer(rmsnorm(h, ab_norm[j]), ab_w_in[j], ab_conv_w[j], ab_conv_b[j],
                             ab_gate_a_w[j], ab_gate_a_b[j], ab_gate_x_w[j], ab_gate_x_b[j],
                             ab_lru_lambda[j], ab_alpha_w[j], ab_alpha_b[j], ab_gla_norm[j],
                             ab_w_out[j])
        else:
            h = h + mla_mixer(rmsnorm(h, c_norm[j]), c_w_in[j], c_q_norm[j], c_w_q_up[j],
                              c_kv_norm[j], c_w_kv_up[j], c_w_out[j], cos, sin)
    h = rmsnorm(h, final_norm)
    return h[:, N_META:]
```

```python
import math
import numpy as np
from contextlib import ExitStack
import concourse.bass as bass
import concourse.mybir as mybir
from concourse.bass_utils import run_bass_kernel_spmd

F32 = mybir.dt.float32
BF16 = mybir.dt.bfloat16
I32 = mybir.dt.int32
AF = mybir.ActivationFunctionType
ALU = mybir.AluOpType

EPOCH = 12000
T = 2064
NMETA = 16
SEQ = 2048
BLKS = [(0, 512), (512, 512), (1024, 512), (1536, 512), (2048, 16)]
EPS = 1e-6
NW = 8
NA = 11400
DBG = {}


ACT_SETS = [(0, ("Exp", "Tanh")), (2, ("Sigmoid", "Tanh")), (3, ("Sqrt",)), (5, ("Ln",)), (9, ("Sin",)),
            (18, ("Silu", "Tanh", "Sin"))]
ACT_FREE = ("Copy", "Identity", "Square")
SEM_LAT = 1100.0
WINDOW = {"pe": 48, "act": 12, "dve": 12, "pool": 1, "sp": 1}


class Prog:
    QUEUES = ("sp", "act", "pool")

    def __init__(self, nc):
        self.nc = nc
        self.ops = {e: [] for e in ("pe", "act", "dve", "pool", "sp")}
        self.chan = {}
        self.lastw = {}
        self.readers = {}
        self.nch = {"sp": 8, "act": 2, "pool": 8}
        self.rr = {q: 0 for q in self.QUEUES}
        self.stack = ExitStack()
        self.cur_fence = {e: None for e in self.ops}
        self.nrec = 0
        self.since_fence = []
        self.do_sched = True

    def sb(self, name, shape, dt):
        return self.stack.enter_context(self.nc.sbuf_tensor(name, list(shape), dt))

    def ps(self, name, shape, dt=F32):
        return self.stack.enter_context(self.nc.psum_tensor(name, list(shape), dt))

    def _emit(self, eng, fn, reads, writes, is_dma=False, extra=(), cost=100.0, func=None, chan=None):
        deps = {}

        def add(r):
            if r is not None:
                deps[r["id"]] = r

        for k in reads:
            add(self.lastw.get(k))
        for k in writes:
            add(self.lastw.get(k))
            for r in self.readers.get(k, ()):
                add(r)
        for r in extra:
            add(r)
        add(self.cur_fence[eng])
        if is_dma:
            cl = self.chan.setdefault(chan, [])
            if cl:
                add(cl[-1])
        rec = {"id": self.nrec, "fn": fn, "eng": eng, "deps": list(deps.values()), "dma": is_dma, "chan": chan,
               "cost": cost, "func": func, "needed": False, "tag": getattr(self, "tag", None)}
        self.nrec += 1
        if is_dma:
            cl.append(rec)
        self.ops[eng].append(rec)
        self.since_fence.append(rec)
        for k in reads:
            self.readers.setdefault(k, []).append(rec)
        for k in writes:
            self.lastw[k] = rec
            self.readers[k] = []
        return rec

    def op(self, eng, fn, reads=(), writes=(), cost=100.0, func=None):
        return self._emit(eng, fn, list(reads), list(writes), cost=cost, func=func)

    def dma(self, q, out, in_, reads=(), writes=(), nbytes=65536):
        c = self.rr[q]
        self.rr[q] = (c + 1) % self.nch[q]
        return self._emit(q, lambda e: e.dma_start(out=out, in_=in_), list(reads), list(writes), is_dma=True,
                          chan=(q, c), cost=2000.0 + nbytes / 150.0)

    def wait_for(self, eng, keys):
        return self._emit(eng, None, list(keys), [], cost=0.0)

    def fence(self):
        allprev = [r for r in self.since_fence if r["fn"] is not None]
        for e in ("act", "dve"):
            r = self._emit(e, None, [], [], extra=allprev, cost=0.0)
            self.cur_fence[e] = r
        self.since_fence = []

    def schedule(self):
        fin = {}
        self.fence_times = []
        order = {e: [] for e in self.ops}
        pend = {e: list(l) for e, l in self.ops.items()}
        tfree = {e: 0.0 for e in self.ops}
        cur_set = [None]

        def act_switch(func):
            if func is None or func in ACT_FREE:
                return None
            if cur_set[0] is not None and func in cur_set[0]:
                return None
            for _, fs in ACT_SETS:
                if func in fs:
                    return fs
            return None

        def candidate(e):
            best = None
            lst = pend[e]
            W = WINDOW[e] if self.do_sched else 1
            seen_fence = False
            for pos in range(min(W, len(lst))):
                r = lst[pos]
                if r["fn"] is None and pos > 0:
                    break
                ok = True
                rdy = 0.0
                for d in r["deps"]:
                    f = fin.get(d["id"])
                    if f is None:
                        ok = False
                        break
                    lat = 0.0 if (d["eng"] == e and e == "pe" and not d["dma"]) else (SEM_LAT if d["eng"] != e or d["dma"] else 60.0)
                    if d["eng"] == e and not d["dma"] and e == "pe":
                        f = f - 150.0 + 20.0
                    if f + lat > rdy:
                        rdy = f + lat
                if ok:
                    st = max(tfree[e], rdy)
                    r["_rdy"] = rdy
                    pen = 1300.0 if (e == "act" and act_switch(r["func"]) is not None) else 0.0
                    key = (st + pen, r["id"])
                    if best is None or key < best[0]:
                        best = (key, pos, st, pen)
                if r["fn"] is None:
                    break
            return best

        remaining = sum(len(l) for l in pend.values())
        cands = {e: candidate(e) for e in self.ops}
        while remaining:
            be = None
            for e, c in cands.items():
                if c is not None and (be is None or c[0] < cands[be][0]):
                    be = e
            assert be is not None, "scheduler deadlock"
            key, pos, st, pen = cands[be]
            r = pend[be].pop(pos)
            r["_st"] = st
            r["_prev"] = order[be][-1] if order[be] else None
            r["_engwait"] = tfree[be] >= r.get("_rdy", 0.0)
            if be == "act":
                fs = act_switch(r["func"])
                if fs is not None:
                    cur_set[0] = fs
            if r["dma"]:
                tfree[be] = st + (900.0 if be == "pool" else 100.0)
                fin[r["id"]] = st + r["cost"]
            else:
                tfree[be] = st + pen + r["cost"]
                fin[r["id"]] = tfree[be] if be != "pe" else tfree[be] + 150.0
            r["_end"] = fin[r["id"]]
            order[be].append(r)
            if r["fn"] is None and be == "act" and len(r["deps"]) > 20:
                self.fence_times.append(st)
            remaining -= 1
            for e in self.ops:
                cands[e] = candidate(e)
        self.sim_time = max(fin.values()) if fin else 0.0
        return order

    def build(self):
        nc = self.nc
        order = self.schedule()
        vlist = {}
        for e, lst in order.items():
            for r in lst:
                v = ("dma",) + r["chan"] if r["dma"] else e
                l = vlist.setdefault(v, [])
                r["veng"] = v
                r["vpos"] = len(l)
                l.append(r)
        for e, lst in order.items():
            known = {}
            for r in lst:
                need = {}
                for d in r["deps"]:
                    v = d["veng"]
                    if v == "pe" and e == "pe" and not r["dma"]:
                        continue
                    if d["fn"] is None:
                        continue
                    if need.get(v, -1) < d["vpos"]:
                        need[v] = d["vpos"]
                waits = []
                for v, p in need.items():
                    if known.get(v, -1) >= p:
                        continue
                    known[v] = p
                    waits.append(vlist[v][p])
                r["waits"] = waits
                for w in waits:
                    w["needed"] = True
        sems = {}
        for v, lst in vlist.items():
            cnt = 0
            for r in lst:
                if r["fn"] is not None and (r["dma"] or r["needed"]):
                    r["sval"] = cnt
                    cnt += 1
                else:
                    r["sval"] = None
            nep = max(1, (cnt + EPOCH - 1) // EPOCH)
            nm = "_".join(map(str, v)) if isinstance(v, tuple) else v
            sems[v] = [self.stack.enter_context(nc.semaphore("s_%s_%d" % (nm, k))) for k in range(nep)]

        def replay(ename, e):
            for r in order[ename]:
                for t in r["waits"]:
                    sv = t["sval"]
                    mult = 16 if t["dma"] else 1
                    e.wait_ge(sems[t["veng"]][sv // EPOCH], (sv % EPOCH + 1) * mult)
                if r["fn"] is None:
                    continue
                ins = r["fn"](e)
                if r["sval"] is not None:
                    sv = r["sval"]
                    ins.then_inc(sems[r["veng"]][sv // EPOCH], 16 if r["dma"] else 1)

        with nc.Block() as block:
            @block.sync
            def _(e):
                replay("sp", e)

            @block.tensor
            def _(e):
                replay("pe", e)

            @block.scalar
            def _(e):
                replay("act", e)

            @block.vector
            def _(e):
                replay("dve", e)

            @block.gpsimd
            def _(e):
                replay("pool", e)
        self.stack.close()


def pcn(w):
    K, n = w.shape
    return np.ascontiguousarray(w.reshape(K // 128, 128, n).transpose(1, 0, 2).reshape(128, -1))


def weight_tiles(inp=None):
    out = []

    def add(key, ncols, fn):
        out.append((key, ncols, fn() if inp is not None else None))

    def pad128(rows):
        a = np.zeros((128, rows.shape[1]), np.float32)
        a[: rows.shape[0]] = rows
        return a

    for layer in range(4):
        j = layer // 2
        if layer % 2 == 0:
            wi = inp["ab_w_in"][j] if inp is not None else None
            wo = inp["ab_w_out"][j] if inp is not None else None
            for g in range(8):
                add(("ab", j, "xa", g), 1024, lambda: pcn(wi[:, 128 * g:128 * g + 128]))
                add(("ab", j, "ga", g), 1024, lambda: pcn(wi[:, 1024 + 128 * g:1024 + 128 * g + 128]))
                add(("ab", j, "gaw", g), 128, lambda: np.ascontiguousarray(inp["ab_gate_a_w"][j, g]))
                add(("ab", j, "gxw", g), 128, lambda: np.ascontiguousarray(inp["ab_gate_x_w"][j, g]))
            for d in range(8):
                add(("ab", j, "woA", d), 1024, lambda: pcn(wo[0:1024, 128 * d:128 * d + 128]))
            add(("ab", j, "ad"), 1024, lambda: pcn(np.concatenate([wi[:, 4096:4112], np.zeros((1024, 112), np.float32)], axis=1)))
            add(("ab", j, "alw"), 512, lambda: pad128(inp["ab_alpha_w"][j]))
            for hd in range(4):
                add(("ab", j, "q", hd), 1024, lambda: pcn(wi[:, 2048 + 128 * hd:2048 + 128 * hd + 128]))
                add(("ab", j, "k", hd), 1024, lambda: pcn(wi[:, 2560 + 128 * hd:2560 + 128 * hd + 128]))
            for c in range(8):
                add(("ab", j, "v", c), 1024, lambda: np.ascontiguousarray(wi[128 * c:128 * c + 128, 3072:4096]))
            for g in range(8):
                add(("ab", j, "gb", g), 1024, lambda: pcn(wi[:, 4112 + 128 * g:4112 + 128 * g + 128]))
            for d in range(8):
                add(("ab", j, "woB", d), 1024, lambda: pcn(wo[1024:2048, 128 * d:128 * d + 128]))
        else:
            wi = inp["c_w_in"][j] if inp is not None else None
            wq = inp["c_w_q_up"][j] if inp is not None else None
            wkv = inp["c_w_kv_up"][j] if inp is not None else None
            wo = inp["c_w_out"][j] if inp is not None else None

            def ropepad(src, c0, swap, nope0=None):
                a = np.zeros((src.shape[0], 128), np.float32)
                if nope0 is not None:
                    a[:, 0:64] = src[:, nope0:nope0 + 64]
                x1 = src[:, c0:c0 + 16]
                x2 = src[:, c0 + 16:c0 + 32]
                a[:, 64:80] = x1
                a[:, 80:96] = x2
                a[:, 96:112] = x2
                a[:, 112:128] = x1
                return a

            for t in range(4):
                add(("c", j, "cq", t), 1024, lambda: pcn(wi[:, 128 * t:128 * t + 128]))
            for t in range(2):
                add(("c", j, "ckv", t), 1024, lambda: pcn(wi[:, 512 + 128 * t:512 + 128 * t + 128]))
            add(("c", j, "kr"), 1024, lambda: pcn(ropepad(wi, 768, False)))
            for g in range(8):
                add(("c", j, "gate", g), 1024, lambda: pcn(wi[:, 800 + 128 * g:800 + 128 * g + 128]))
            for hh in range(16):
                add(("c", j, "q", hh), 512, lambda: pcn(ropepad(wq, 96 * hh + 64, False, nope0=96 * hh)))
                if hh % 2 == 0:
                    add(("c", j, "kn2", hh // 2), 256, lambda: pcn(np.concatenate(
                        [wkv[:, 128 * hh:128 * hh + 64], wkv[:, 128 * (hh + 1):128 * (hh + 1) + 64]], axis=1)))
                add(("c", j, "vv", hh), 128, lambda: pcn(wkv[:, 128 * hh + 64:128 * hh + 128]))
            for d in range(8):
                add(("c", j, "wo", d), 1024, lambda: pcn(wo[:, 128 * d:128 * d + 128]))
    return out


def col(v):
    return np.ascontiguousarray(np.asarray(v, np.float32).reshape(-1, 128).T)


def const_cols(inp=None):
    out = []

    def add(key, ncols, fn):
        out.append((key, ncols, fn() if inp is not None else None))

    for j in range(2):
        add(("abn", j), 8, lambda: col(inp["ab_norm"][j]))
        add(("convw", j), 32, lambda: np.ascontiguousarray(
            inp["ab_conv_w"][j].reshape(4, 8, 128).transpose(2, 1, 0).reshape(128, 32)))
        add(("convb", j), 8, lambda: col(inp["ab_conv_b"][j]))
        add(("gab", j), 8, lambda: col(inp["ab_gate_a_b"][j]))
        add(("gxb", j), 8, lambda: col(inp["ab_gate_x_b"][j]))
        add(("lam", j), 8, lambda: col(inp["ab_lru_lambda"][j]))
        add(("alb", j), 4, lambda: col(inp["ab_alpha_b"][j]))
        add(("gln", j), 2, lambda: col(inp["ab_gla_norm"][j]))
        add(("cn", j), 8, lambda: col(inp["c_norm"][j]))
        add(("qn", j), 4, lambda: col(inp["c_q_norm"][j]))
        add(("kvn", j), 2, lambda: col(inp["c_kv_norm"][j]))
    add(("fn",), 8, lambda: col(inp["final_norm"]))

    def invf():
        a = np.zeros((128, 1), np.float32)
        f = (10000.0 ** (-np.arange(0, 32, 2, dtype=np.float32) / 32.0)).astype(np.float32)
        for r0 in (64, 80, 96, 112):
            a[r0:r0 + 16, 0] = f
        return a

    def sgn():
        a = np.zeros((128, 1), np.float32)
        a[96:112, 0] = -1.0
        a[112:128, 0] = 1.0
        return a

    add(("invf",), 1, invf)
    add(("sgn",), 1, sgn)
    add(("metapos",), 16, lambda: np.tile(np.arange(16, dtype=np.float32)[None], (128, 1)))
    add(("tri",), 128, lambda: np.triu(np.ones((128, 128), np.float32)))
    add(("ident",), 128, lambda: np.eye(128, dtype=np.float32))
    add(("negtri",), 128, lambda: np.tril(np.full((128, 128), -30000.0, np.float32), k=-1))

    def rmask():
        a = np.ones((128, 512), np.float32)
        a[:, ::128] = 0.0
        return a

    add(("rmask",), 512, rmask)
    return out


def offsets(lst):
    off = {}
    o = 0
    for key, n, _ in lst:
        off[key] = (o, n)
        o += n
    return off, o


class Builder:
    def __init__(self, nlayers=4, final=True):
        self.nlayers = nlayers
        self.final = final
        nc = bass.Bass("TRN2", target_bir_lowering=False)
        self.nc = nc
        P = self.P = Prog(nc)
        self.woff, wtot = offsets(weight_tiles())
        self.coff, ctot = offsets(const_cols())
        self.xT = nc.dram_tensor("xT", [1024, SEQ], F32, kind="ExternalInput").ap()
        self.metaT = nc.dram_tensor("metaT", [1024, NMETA], F32, kind="ExternalInput").ap()
        self.posr = nc.dram_tensor("posr", [128, SEQ], I32, kind="ExternalInput").ap()
        self.wts = nc.dram_tensor("wts", [128, wtot], F32, kind="ExternalInput").ap()
        self.cst = nc.dram_tensor("cst", [128, ctot], F32, kind="ExternalInput").ap()
        self.outT = nc.dram_tensor("outT", [1024, SEQ if final else T], F32, kind="ExternalOutput").ap()

        self.h = P.sb("h", [128, 8 * T], F32)[:, :].rearrange("p (c t) -> p c t", c=8)
        self.hn = P.sb("hn", [128, 8 * T], BF16)[:, :].rearrange("p (c t) -> p c t", c=8)
        self.mix2d = P.sb("mix", [128, 8 * T], BF16)[:, :]
        self.mix = self.mix2d.rearrange("p (c t) -> p c t", c=8)
        self.Ct = P.sb("Ct", [128, T], BF16)[:, :]
        self.St = P.sb("St", [128, T], BF16)[:, :]
        self.C = P.sb("cst_sb", [128, ctot], F32)
        self.cbf = P.sb("cbf", [128, 512], BF16)
        self.wp = [P.sb("wp%d" % i, [128, 1024], BF16) for i in range(NW)]
        self.wi = 0
        self.arena = P.sb("arena", [128, NA], F32)
        self.banks = [P.ps("ps%d" % i, [128, 512], F32) for i in range(8)]
        self.bi = 0
        self.gctr = {}
        self.dumpkeys = []
        self.apos = 0

    def cc(self, key, c0=0, n=1, rows=slice(0, 128)):
        o, _ = self.coff[key]
        return self.C[rows, o + c0:o + c0 + n]

    def bank(self, grp=None):
        if grp is None:
            i = self.bi
            self.bi = (i + 1) % 8
        else:
            lst, name = grp
            k = self.gctr.get(name, 0)
            self.gctr[name] = k + 1
            i = lst[k % len(lst)]
        return self.banks[i], "ps%d" % i

    def areset(self):
        self.apos = 0

    def af32(self, n):
        v = self.arena[:, self.apos:self.apos + n]
        self.apos += n
        assert self.apos <= NA, self.apos
        return v

    def abf(self, n):
        assert n % 2 == 0
        return self.af32(n // 2).bitcast(BF16)

    def wload(self, key):
        o, n = self.woff[key]
        i = self.wi
        self.wi = (i + 1) % NW
        buf = self.wp[i]
        self.P.dma("pool", buf[:, 0:n], self.wts[:, o:o + n], writes=["wp%d" % i], nbytes=n * 512)
        return buf, "wp%d" % i

    @staticmethod
    def fsz(ap):
        n = 1
        for d in ap.shape[1:]:
            n *= d
        return n

    def mm(self, out, lhsT, rhs, start, stop, reads, writes):
        n = self.fsz(rhs)
        self.P.op("pe", lambda e: e.matmul(out, lhsT=lhsT, rhs=rhs, start=start, stop=stop), reads, writes,
                  cost=max(64, n) / 1.95 + 12.0)

    def act(self, out, in_, func, reads, writes, scale=1.0, bias=0.0):
        self.P.op("act", lambda e: e.activation(out=out, in_=in_, func=func, bias=bias, scale=scale), reads, writes,
                  cost=185.0 + self.fsz(out) / 1.2, func=func.name)

    def dcost(self, out, f):
        return 65.0 + f * self.fsz(out) / 0.96

    def tt(self, out, in0, in1, op, reads, writes, eng="dve"):
        self.P.op(eng, lambda e: e.tensor_tensor(out=out, in0=in0, in1=in1, op=op), reads, writes,
                  cost=self.dcost(out, 1.0 if eng == "dve" else 2.0))

    def ts(self, out, in0, s1, s2, op0, op1, reads, writes, eng="dve"):
        self.P.op(eng, lambda e: e.tensor_scalar(out=out, in0=in0, scalar1=s1, scalar2=s2, op0=op0, op1=op1), reads, writes,
                  cost=self.dcost(out, 0.6))

    def stt(self, out, in0, scalar, in1, op0, op1, reads, writes):
        self.P.op("dve", lambda e: e.scalar_tensor_tensor(out=out, in0=in0, scalar=scalar, in1=in1, op0=op0, op1=op1), reads, writes,
                  cost=self.dcost(out, 1.0))

    def cp(self, out, in_, reads, writes, eng="dve"):
        self.P.op(eng, lambda e: e.tensor_copy(out=out, in_=in_), reads, writes, cost=self.dcost(out, 0.6))

    def recip(self, out, in_, reads, writes):
        self.P.op("dve", lambda e: e.reciprocal(out=out, in_=in_), reads, writes, cost=self.dcost(out, 1.0))

    def scan(self, out, d0, d1, init, reads, writes):
        self.P.op("dve", lambda e: e.tensor_tensor_scan(out=out, data0=d0, data1=d1, initial=init, op0=ALU.mult, op1=ALU.add), reads, writes,
                  cost=self.dcost(out, 2.0))

    def memset(self, ap, val, writes, eng="dve"):
        self.P.op(eng, lambda e: e.memset(ap, val), [], writes, cost=self.dcost(ap, 0.5))

    def dump(self, name, ap, keys):
        shp = list(ap.shape)
        d = self.nc.dram_tensor("dbg_" + name, shp, ap.dtype, kind="ExternalOutput").ap()
        k = ("dbg", name)
        self.P.dma("sp", d, ap, reads=keys, writes=[k])
        self.dumpkeys.append(k)

    def lin(self, pb, pk, M, wbuf, wkey, ncol, c0, src, srckeys, b0, n, nk=8, m0=0):
        for c in range(nk):
            self.mm(pb[m0:m0 + M, 0:n], wbuf[:, c * ncol + c0:c * ncol + c0 + M], src[:, c, b0:b0 + n],
                    c == 0, c == nk - 1, [wkey] + srckeys, [pk])

    def prologue(self):
        P = self.P
        P.dma("sp", self.C[:, :], self.cst[:, :], writes=["C"])
        for c in range(8):
            P.dma("sp", self.h[:, c, 0:NMETA], self.metaT[c * 128:(c + 1) * 128, :], writes=[("h", c, 0)], nbytes=8192)
        for bi, (b0, n) in enumerate(BLKS):
            for c in range(8):
                lo = max(b0, NMETA)
                P.dma("sp", self.h[:, c, lo:b0 + n], self.xT[c * 128:(c + 1) * 128, lo - NMETA:b0 + n - NMETA],
                      writes=[("h", c, bi)] if bi > 0 else [("h", c, 0, "x")], nbytes=(b0 + n - lo) * 512)
        o, _ = self.coff[("tri",)]
        self.cp(self.cbf[:, 0:384], self.C[:, o:o + 384], ["C"], ["cbf"])
        self.memset(self.cbf[:, 384:512], 1.0, ["cbf1"])
        self.tri = self.cbf[:, 0:128]
        self.ident = self.cbf[:, 128:256]
        self.negtri = self.cbf[:, 256:384]
        self.ones = self.cbf[:, 384:512]

    def hkeys(self, c, bi):
        return [("h", c, bi)] + ([("h", c, 0, "x")] if bi == 0 else [])

    def rmsnorm_h(self, gkey, dst_fn):
        for bi in range(len(BLKS)):
            self.rmsnorm_block(gkey, dst_fn, bi)

    def rmsnorm_block(self, gkey, dst_fn, bi):
        if True:
            b0, n = BLKS[bi]
            pb, pk = self.bank(([6, 7], "norm"))
            d = bi % 2
            sq, nsd, nrs = self.nsq[d], self.nsd[d], self.nrs[d]
            for c in range(8):
                self.act(sq[:, c, 0:n], self.h[:, c, b0:b0 + n], AF.Square, self.hkeys(c, bi), [("nsq", d, c)])
            for c in range(8):
                self.mm(pb[:, 0:n], self.ones, sq[:, c, 0:n], c == 0, c == 7, ["cbf1", ("nsq", d, c)], [pk])
            self.act(nsd[:, 0:n], pb[:, 0:n], AF.Sqrt, [pk, "epsc"], [("nsd", d)], scale=1.0 / 1024.0, bias=self.epsc)
            self.recip(nrs[:, 0:n], nsd[:, 0:n], [("nsd", d)], [("nrs", d)])
            for c in range(8):
                o, w = dst_fn(c, bi, b0, n)
                self.stt(o, self.h[:, c, b0:b0 + n], self.cc(gkey, c), nrs[:, 0:n], ALU.mult, ALU.mult,
                         self.hkeys(c, bi) + [("nrs", d), "C"], w)

    def norm_bufs(self):
        self.nsq = [self.abf(8 * 512).rearrange("p (c t) -> p c t", c=8) for _ in range(2)]
        self.nsd = [self.af32(512) for _ in range(2)]
        self.nrs = [self.af32(512) for _ in range(2)]

    def residual(self, wkind, j, srckeyfn):
        for d in range(8):
            wb, wk = self.wload((wkind[0], j, wkind[1], d))
            for bi, (b0, n) in enumerate(BLKS):
                pb, pk = self.bank()
                for c in range(8):
                    self.mm(pb[:, 0:n], wb[:, c * 128:(c + 1) * 128], self.mix[:, c, b0:b0 + n], c == 0, c == 7,
                            [wk] + srckeyfn(c, bi), [pk])
                self.tt(self.h[:, d, b0:b0 + n], self.h[:, d, b0:b0 + n], pb[:, 0:n], ALU.add,
                        self.hkeys(d, bi) + [pk], [("h", d, bi)] + ([("h", d, 0, "x")] if bi == 0 else []))

    def residual_bo(self, wkind, j, srckeyfn, post=None):
        wt = [self.wload((wkind[0], j, wkind[1], d)) for d in range(8)]
        for bi, (b0, n) in enumerate(BLKS):
            for d in range(8):
                wb, wk = wt[d]
                pb, pk = self.bank(([0, 1, 2, 3, 4, 5], "res"))
                for c in range(8):
                    self.mm(pb[:, 0:n], wb[:, c * 128:(c + 1) * 128], self.mix[:, c, b0:b0 + n], c == 0, c == 7,
                            [wk] + srckeyfn(c, bi), [pk])
                self.tt(self.h[:, d, b0:b0 + n], self.h[:, d, b0:b0 + n], pb[:, 0:n], ALU.add,
                        self.hkeys(d, bi) + [pk], [("h", d, bi)] + ([("h", d, 0, "x")] if bi == 0 else []))
            if post is not None:
                post(bi)

    def norm_for(self, kind, bi):
        if kind is None:
            if self.final:
                self.final_block(bi)
            return
        gkey = ("abn", kind[1]) if kind[0] == "ab" else ("cn", kind[1])
        self.rmsnorm_block(gkey, lambda c, bi_, b0, n: (self.hn[:, c, b0:b0 + n], [("hn", c, bi_)]), bi)

    def final_block(self, bi):
        P = self.P

        def dst(c, bi_, b0, n):
            i = self.otc % 4
            self.otc += 1
            self._cur = (c, bi_, b0, n)
            return self.ot[i][:, 0:n], ["ot%d" % i]

        orig_stt = self.stt

        def stt2(out, in0, scalar, in1, op0, op1, reads, writes):
            orig_stt(out, in0, scalar, in1, op0, op1, reads, writes)
            c, bi_, b0, n = self._cur
            lo = max(b0, NMETA)
            k = ("out", c, bi_)
            P.dma("sp", self.outT[c * 128:(c + 1) * 128, lo - NMETA:b0 + n - NMETA], out[:, lo - b0:n], reads=writes, writes=[k])
            self.okeys.append(k)

        self.stt = stt2
        self.rmsnorm_block(("fn",), dst, bi)
        self.stt = orig_stt

    def begin_layer(self, nxt):
        self.P.fence()
        self.areset()
        if nxt is not None and nxt[0] == "ab":
            self.ab_consts(nxt[1])
        self.norm_bufs()
        if nxt is None:
            self.ot = [self.af32(512) for _ in range(4)]

    def ab_consts(self, j):
        cpc = self.af32(8)
        cp2 = self.af32(8)
        nalb = self.af32(4)
        self.act(cpc, self.cc(("lam", j), 0, 8), AF.Exp, ["C"], ["cpc"], scale=-1.0)
        self.act(cpc, cpc, AF.Ln, ["cpc", "onec"], ["cpc"], bias=self.onec)
        self.ts(cp2, cpc, -16.0, None, ALU.mult, ALU.bypass, ["cpc"], ["cp2"])
        self.ts(cpc, cpc, -8.0, None, ALU.mult, ALU.bypass, ["cpc"], ["cpc"])
        self.ts(nalb, self.cc(("alb", j), 0, 4), -1.0, None, ALU.mult, ALU.bypass, ["C"], ["nalb"])
        self.abc = (cpc, cp2, nalb)
        self.amark = self.apos

    def layer_ab(self, j, nxt_layer):
        P = self.P
        cpc, cp2, nalb = self.abc
        amark = self.amark
        hnk = lambda bi: [("hn", c, bi) for c in range(8)]
        P.fence()
        self.apos = amark
        HN = 1040
        xb = [self.abf(4 + 512) for _ in range(2)]
        dg = self.abf(4 * 128).rearrange("p (k m) -> p k m", k=4)
        cvb = [self.abf(512) for _ in range(2)]
        tr = [self.af32(512) for _ in range(2)]
        ti = [self.af32(512) for _ in range(2)]
        tg = self.af32(512)
        Ah = [self.af32(HN) for _ in range(2)]
        Uh = [self.af32(HN) for _ in range(2)]
        Wh = [self.abf(HN) for _ in range(2)]
        A2 = self.af32(HN)
        Hs = self.af32(HN)
        hl = self.af32(1)
        hc = self.af32(8)
        hba = self.af32(8)
        hbx = self.af32(8)
        self.ts(hc, cpc, 0.5, None, ALU.mult, ALU.bypass, ["cpc"], ["hc"])
        self.ts(hba, self.cc(("gab", j), 0, 8), 0.5, None, ALU.mult, ALU.bypass, ["C"], ["hba"])
        self.ts(hbx, self.cc(("gxb", j), 0, 8), 0.5, None, ALU.mult, ALU.bypass, ["C"], ["hbx"])
        o_id, _ = self.coff[("ident",)]
        cnt = 0
        hcnt = 0
        for g in range(8):
            wxa, kxa = self.wload(("ab", j, "xa", g))
            wga, kga = self.wload(("ab", j, "ga", g))
            wra, kra = self.wload(("ab", j, "gaw", g))
            wrx, krx = self.wload(("ab", j, "gxw", g))
            for k in range(4):
                self.ts(dg[:, k, :], self.C[:, o_id:o_id + 128], self.cc(("convw", j), g * 4 + k), None, ALU.mult, ALU.bypass,
                        ["C"], [("dg", k)])
            self.memset(xb[0][:, 0:3], 0.0, ["xb0h"])
            for hi, hblks in enumerate(((0, 1), (2, 3, 4))):
                hp = hcnt % 2
                hcnt += 1
                A_, U_, W_ = Ah[hp], Uh[hp], Wh[hp]
                kA, kU, kW = "Ah%d" % hp, "Uh%d" % hp, "Wh%d" % hp
                off = 0
                for bi in hblks:
                    b0, n = BLKS[bi]
                    cur, nxt = bi % 2, (bi + 1) % 2
                    d = cnt % 2
                    cnt += 1
                    px, kx = self.bank()
                    pg, kg = self.bank()
                    self.lin(px, kx, 128, wxa, kxa, 128, 0, self.hn, hnk(bi), b0, n)
                    self.lin(pg, kg, 128, wga, kga, 128, 0, self.hn, hnk(bi), b0, n)
                    X = xb[cur]
                    xk = "xb%d" % cur
                    self.cp(X[:, 3:3 + n], px[:, 0:n], [kx], [xk])
                    if bi + 1 < len(BLKS):
                        self.cp(xb[nxt][:, 0:3], X[:, n:n + 3], [xk], ["xb%dh" % nxt])
                    pc, kc = self.bank()
                    for k in range(4):
                        self.mm(pc[:, 0:n], dg[:, k, :], X[:, k:k + n], k == 0, k == 3, [("dg", k), xk, xk + "h"], [kc])
                    self.act(cvb[d][:, 0:n], pc[:, 0:n], AF.Identity, [kc, "C"], ["cvb%d" % d], bias=self.cc(("convb", j), g))
                    pr, kr = self.bank()
                    pi, ki = self.bank()
                    self.mm(pr[:, 0:n], wra[:, 0:128], cvb[d][:, 0:n], True, True, [kra, "cvb%d" % d], [kr])
                    self.mm(pi[:, 0:n], wrx[:, 0:128], cvb[d][:, 0:n], True, True, [krx, "cvb%d" % d], [ki])
                    self.act(tr[d][:, 0:n], pr[:, 0:n], AF.Tanh, [kr, "hba"], ["tr%d" % d], scale=0.5, bias=hba[:, g:g + 1])
                    self.act(ti[d][:, 0:n], pi[:, 0:n], AF.Tanh, [ki, "hbx"], ["ti%d" % d], scale=0.5, bias=hbx[:, g:g + 1])
                    self.act(A_[:, off:off + n], tr[d][:, 0:n], AF.Exp, ["tr%d" % d, "hc"], [(kA, bi)], scale=hc[:, g:g + 1], bias=hc[:, g:g + 1])
                    self.stt(U_[:, off:off + n], ti[d][:, 0:n], 1.0, cvb[d][:, 0:n], ALU.add, ALU.mult, ["ti%d" % d, "cvb%d" % d], [(kU, bi)])
                    self.act(tg[:, 0:n], pg[:, 0:n], AF.Tanh, [kg], ["tg"], scale=0.5)
                    self.stt(W_[:, off:off + n], tg[:, 0:n], 1.0, pg[:, 0:n], ALU.add, ALU.mult, ["tg", kg], [(kW, bi)])
                    off += n
                N = off
                t0 = BLKS[hblks[0]][0]
                akeys = [(kA, bi) for bi in hblks]
                ukeys = [(kU, bi) for bi in hblks]
                wkeys = [(kW, bi) for bi in hblks]
                self.act(A2[:, 0:N], A_[:, 0:N], AF.Square, akeys, ["A2"])
                self.act(A2[:, 0:N], A2[:, 0:N], AF.Sqrt, ["A2", "onec"], ["A2"], scale=-1.0, bias=self.onec)
                self.stt(U_[:, 0:N], U_[:, 0:N], 0.5, A2[:, 0:N], ALU.mult, ALU.mult, ukeys + ["A2"], ukeys)
                if hi == 0:
                    init, ird = 0.0, []
                else:
                    init, ird = hl[:, 0:1], ["hl"]
                self.scan(Hs[:, 0:N], A_[:, 0:N], U_[:, 0:N], init, akeys + ukeys + ird, ["Hs"])
                if hi == 0:
                    self.cp(hl[:, 0:1], Hs[:, N - 1:N], ["Hs"], ["hl"])
                self.stt(self.mix[:, g, t0:t0 + N], Hs[:, 0:N], 0.5, W_[:, 0:N], ALU.mult, ALU.mult, ["Hs"] + wkeys,
                         [("mix", g, bi) for bi in hblks])
        if not DBG.get("skipA"):
            self.residual(("ab", "woA"), j, lambda c, bi: [("mix", c, bi)])
        P.fence()
        self.apos = amark
        adT = self.abf(512)
        qd = [self.abf(512) for _ in range(2)]
        kinv = [self.abf(512) for _ in range(2)]
        kend = [self.abf(512) for _ in range(2)]
        vtok = self.abf(4 * 1024).rearrange("p (t f) -> p t f", t=4)
        sgbs = [self.abf(2 * 512).rearrange("p (g t) -> p g t", g=2) for _ in range(2)]
        obs = [self.af32(2 * 512).rearrange("p (g t) -> p g t", g=2) for _ in range(2)]
        S = self.af32(4 * 256).rearrange("p (h e) -> p h e", h=4)
        Sb = self.abf(4 * 256).rearrange("p (h e) -> p h e", h=4)
        ebl = self.af32(16).rearrange("p (h c) -> p h c", h=4)
        t1 = self.af32(512)
        t2 = self.af32(512)
        eb = self.af32(512)
        t4 = self.af32(512)
        stm = [self.abf(128) for _ in range(2)]
        ktk = [self.abf(128) for _ in range(2)]
        osqs = [self.abf(2 * 512).rearrange("p (e t) -> p e t", e=2)]
        self.memset(S[:, :, :], 0.0, [("S", hd) for hd in range(4)])
        self.memset(Sb[:, :, :], 0.0, [("Sb", hd) for hd in range(4)])
        for bi, (b0, n) in enumerate(BLKS):
            nch = (n + 127) // 128
            wad, kad = self.wload(("ab", j, "ad"))
            pa, ka = self.bank()
            self.lin(pa, ka, 128, wad, kad, 128, 0, self.hn, hnk(bi), b0, n)
            self.act(adT[0:16, 0:n], pa[0:16, 0:n], AF.Copy, [ka], ["adT"])
            P.tag = ("B", bi, -1, "v")
            pvs = {}
            for tt_ in range(nch):
                for hf in range(2):
                    pvs[(tt_, hf)] = self.bank()
            for c in range(8):
                wvc, kwv = self.wload(("ab", j, "v", c))
                for tt_ in range(nch):
                    tn = min(128, n - tt_ * 128)
                    for hf in range(2):
                        pv, kpv = pvs[(tt_, hf)]
                        self.mm(pv[0:tn, 0:512], self.hn[:, c, b0 + tt_ * 128:b0 + tt_ * 128 + tn],
                                wvc[:, hf * 512:(hf + 1) * 512], c == 0, c == 7, [kwv, ("hn", c, bi)], [kpv])
            for tt_ in range(nch):
                tn = min(128, n - tt_ * 128)
                for hf in range(2):
                    pv, kpv = pvs[(tt_, hf)]
                    self.act(vtok[0:tn, tt_, hf * 512:(hf + 1) * 512], pv[0:tn, 0:512], AF.Copy, [kpv], [("vtok", tt_, hf)])
            def st_pre(hd):
                par = hd % 2
                QD, KI, KE = qd[par], kinv[par], kend[par]
                kQD, kKI = "qd%d" % par, "kinv%d" % par
                ob, osq, kob = obs[par], osqs[0], "ob%d" % par
                sgb = sgbs[par]
                walw, kalw = self.wload(("ab", j, "alw"))
                wq, kq = self.wload(("ab", j, "q", hd))
                wk, kk = self.wload(("ab", j, "k", hd))
                pq, kpq = self.bank()
                pk, kpk = self.bank()
                pz, kpz = self.bank()
                self.lin(pq, kpq, 128, wq, kq, 128, 0, self.hn, hnk(bi), b0, n)
                self.lin(pk, kpk, 128, wk, kk, 128, 0, self.hn, hnk(bi), b0, n)
                self.mm(pz[:, 0:n], walw[0:16, hd * 128:(hd + 1) * 128], adT[0:16, 0:n], True, True, [kalw, "adT"], [kpz])
                self.act(t1[:, 0:n], pz[:, 0:n], AF.Exp, [kpz, "nalb"], ["t1"], scale=-1.0, bias=nalb[:, hd:hd + 1])
                self.act(t1[:, 0:n], t1[:, 0:n], AF.Ln, ["t1", "onec"], ["t1"], bias=self.onec)
                self.scan(t2[:, 0:n], self.cc(("rmask",), 0, n), t1[:, 0:n], 0.0, ["t1", "C"], ["t2"])
                self.act(eb[:, 0:n], t2[:, 0:n], AF.Exp, ["t2"], ["eb"], scale=-1.0 / 16.0)
                self.act(t1[:, 0:n], t2[:, 0:n], AF.Exp, ["t2"], ["t1"], scale=1.0 / 16.0)
                self.stt(QD[:, 0:n], pq[:, 0:n], 128.0 ** -0.5, eb[:, 0:n], ALU.mult, ALU.mult, [kpq, "eb"], [kQD])
                self.tt(KI[:, 0:n], pk[:, 0:n], t1[:, 0:n], ALU.mult, [kpk, "t1"], [kKI])
                for ci in range(nch):
                    if n - ci * 128 < 128:
                        continue
                    lc = ci * 128 + 127
                    self.cp(ebl[:, hd, ci:ci + 1], eb[:, lc:lc + 1], ["eb"], [("ebl", hd, ci)])
                    self.ts(KE[:, ci * 128:ci * 128 + 128], KI[:, ci * 128:ci * 128 + 128],
                            ebl[:, hd, ci:ci + 1], None, ALU.mult, ALU.bypass, [kKI, ("ebl", hd, ci)], [("kend", par, ci)])

            def st_gb(hd):
                par = hd % 2
                QD, KI, KE = qd[par], kinv[par], kend[par]
                kQD, kKI = "qd%d" % par, "kinv%d" % par
                ob, osq, kob = obs[par], osqs[0], "ob%d" % par
                sgb = sgbs[par]
                P.tag = ("B", bi, hd, "gb")
                for et in range(2):
                    wg, kg_ = self.wload(("ab", j, "gb", 2 * hd + et))
                    pgb, kpg = self.bank()
                    self.lin(pgb, kpg, 128, wg, kg_, 128, 0, self.hn, hnk(bi), b0, n)
                    self.act(sgb[:, et, 0:n], pgb[:, 0:n], AF.Silu, [kpg], [("sgb", par, et)])

            def st_chunk(hd, ci):
                par = hd % 2
                QD, KI, KE = qd[par], kinv[par], kend[par]
                kQD, kKI = "qd%d" % par, "kinv%d" % par
                ob, osq, kob = obs[par], osqs[0], "ob%d" % par
                sgb = sgbs[par]
                P.tag = ("B", bi, hd, "chunk")
                cn = min(128, n - ci * 128)
                cs = slice(ci * 128, ci * 128 + cn)
                first = (bi == 0 and ci == 0)
                last = (cn < 128)
                si = par
                pst, kst = self.bank()
                self.mm(pst[0:cn, 0:cn], KI[:, cs], QD[:, cs], True, True, [kKI, kQD], [kst])
                self.tt(stm[si][0:cn, 0:cn], pst[0:cn, 0:cn], self.cc(("tri",), 0, cn, slice(0, cn)), ALU.mult,
                        [kst, "C"], ["stm%d" % si])
                if not last:
                    ptr, ktr = self.bank()
                    ptb = ptr[:, 0:64].bitcast(BF16)
                    self.P.op("pe", (lambda o_, i_, id_: (lambda e: e.transpose(o_, i_, id_)))(ptb[:, 0:128], KE[:, cs], self.ident),
                              [("kend", par, ci), "cbf"], [ktr], cost=70.0)
                    self.act(ktk[si][:, 0:128], ptb[:, 0:128], AF.Copy, [ktr], ["ktk%d" % si])
                po, kpo = self.bank()
                for et in range(2):
                    vcol = hd * 256 + et * 128
                    self.mm(po[:, et * 128:et * 128 + cn], vtok[0:cn, ci, vcol:vcol + 128],
                            stm[si][0:cn, 0:cn], True, first, [("vtok", ci, 0), ("vtok", ci, 1), "stm%d" % si], [kpo])
                    if not first:
                        self.mm(po[:, et * 128:et * 128 + cn], Sb[:, hd, et * 128:(et + 1) * 128], QD[:, cs],
                                False, True, [("Sb", hd), kQD], [kpo])
                self.act(ob[:, 0:2, cs], po[:, 0:256].rearrange("p (e c) -> p e c", e=2)[:, :, 0:cn], AF.Copy,
                         [kpo], [kob])
                if not last:
                    pkv, kkv = self.bank()
                    self.mm(pkv[:, 0:256], ktk[si][:, 0:128], vtok[:, ci, hd * 256:(hd + 1) * 256], True, True,
                            ["ktk%d" % si, ("vtok", ci, 0), ("vtok", ci, 1)], [kkv])
                    self.stt(S[:, hd, :], S[:, hd, :], ebl[:, hd, ci:ci + 1], pkv[:, 0:256], ALU.mult, ALU.add,
                             [("S", hd), ("ebl", hd, ci), kkv], [("S", hd)])
                    self.act(Sb[:, hd, :], S[:, hd, :], AF.Copy, [("S", hd)], [("Sb", hd)])

            def st_post(hd):
                par = hd % 2
                QD, KI, KE = qd[par], kinv[par], kend[par]
                kQD, kKI = "qd%d" % par, "kinv%d" % par
                ob, osq, kob = obs[par], osqs[0], "ob%d" % par
                sgb = sgbs[par]
                if DBG.get("dumpB") and bi == DBG.get("dbi", 0) and hd == DBG.get("dhd", 0):
                    self.dump("ob0", ob[:, 0, :], [kob])
                    self.dump("ob1", ob[:, 1, :], [kob])
                    self.dump("qd", QD, [kQD])
                    self.dump("ki", KI, [kKI])
                    self.dump("ke", KE, [("kend", par, c_) for c_ in range(4)])
                    self.dump("S", S[:, hd, :], [("S", hd)])
                    self.dump("vt", vtok[:, 0, :], [("vtok", 0, 0), ("vtok", 0, 1)])
                    self.dump("ebl", ebl[:, hd, :], [("ebl", hd, c_) for c_ in range(4)])
                    self.dump("stm", stm[1], ["stm1"])
                    self.dump("ktk", ktk[1], ["ktk1"])
                P.tag = ("B", bi, hd, "post")
                pss, kss = self.bank()
                for et in range(2):
                    self.act(osq[:, et, 0:n], ob[:, et, 0:n], AF.Square, [kob], [("osq", et)])
                for et in range(2):
                    self.mm(pss[:, 0:n], self.ones, osq[:, et, 0:n], et == 0, et == 1, ["cbf1", ("osq", et)], [kss])
                self.act(t4[:, 0:n], pss[:, 0:n], AF.Sqrt, [kss, "epsc"], ["t4"], scale=1.0 / 256.0, bias=self.epsc)
                self.recip(t4[:, 0:n], t4[:, 0:n], ["t4"], ["t4"])
                for et in range(2):
                    g = 2 * hd + et
                    self.stt(ob[:, et, 0:n], ob[:, et, 0:n], self.cc(("gln", j), et), t4[:, 0:n], ALU.mult, ALU.mult,
                             [kob, "t4", "C"], [kob])
                    self.tt(self.mix[:, g, b0:b0 + n], ob[:, et, 0:n], sgb[:, et, 0:n], ALU.mult, [kob, ("sgb", par, et)], [("mix", g, bi)])
                    if DBG.get("dumpB") and bi == DBG.get("dbi", 0) and hd == DBG.get("dhd", 0):
                        self.dump("mix%d" % et, self.mix[:, g, b0:b0 + n], [("mix", g, bi)])
                        self.dump("sgb%d" % et, sgb[:, et, 0:n], [("sgb", par, et)])
                        if et == 1:
                            self.dump("rstd", t4[:, 0:n], ["t4"])


            for hp in range(2):
                hds = (2 * hp, 2 * hp + 1)
                for hd in hds:
                    st_pre(hd)
                for hd in hds:
                    st_gb(hd)
                for ci in range(nch):
                    for hd in hds:
                        st_chunk(hd, ci)
                for hd in hds:
                    st_post(hd)

        if DBG.get("dumpmix"):
            for g in range(8):
                self.dump("fm%d" % g, self.mix[:, g, 512:1024], [("mix", g, 1)])
        self.begin_layer(nxt_layer)
        if not DBG.get("skipB"):
            self.residual_bo(("ab", "woB"), j, lambda c, bi: [("mix", c, bi)], post=lambda bi: self.norm_for(nxt_layer, bi))

    def rope_tables(self):
        scr = self.mix2d.bitcast(F32)
        ang = scr[:, 0:T]
        kq = scr[:, T:2 * T]
        tab = scr[:, 2 * T:3 * T]
        pi_ = scr[:, 3 * T:3 * T + SEQ].bitcast(I32)
        self.P.dma("sp", pi_, self.posr[:, :], writes=["posi"])
        self.cp(ang[:, NMETA:T], pi_, ["posi"], ["ang"])
        self.ts(ang[:, NMETA:T], ang[:, NMETA:T], float(NMETA), None, ALU.add, ALU.bypass, ["ang"], ["ang"])
        self.ts(ang[:, NMETA:T], ang[:, NMETA:T], self.cc(("invf",)), None, ALU.mult, ALU.bypass, ["ang", "C"], ["ang"])
        self.ts(ang[:, 0:NMETA], self.cc(("metapos",), 0, 16), self.cc(("invf",)), None, ALU.mult, ALU.bypass, ["C"], ["angm"])
        MAGIC = 12582912.0
        C1 = 6.28125
        C2 = 2.0 * math.pi - 6.28125
        for dst, shift, key in ((self.St, 0.0, "St"), (self.Ct, math.pi / 2.0, "Ct")):
            if shift != 0.0:
                self.ts(ang, ang, shift, None, ALU.add, ALU.bypass, ["ang", "angm"], ["ang", "angm"])
            self.ts(kq, ang, 1.0 / (2.0 * math.pi), MAGIC, ALU.mult, ALU.add, ["ang", "angm"], ["kq"])
            self.ts(kq, kq, MAGIC, None, ALU.subtract, ALU.bypass, ["kq"], ["kq"])
            self.stt(tab, kq, -C1, ang, ALU.mult, ALU.add, ["kq", "ang", "angm"], ["tab"])
            self.stt(tab, kq, -C2, tab, ALU.mult, ALU.add, ["kq", "tab"], ["tab"])
            self.ts(tab, tab, 3.141592, -3.141592, ALU.min, ALU.max, ["tab"], ["tab"])
            self.act(tab, tab, AF.Sin, ["tab"], ["tab"])
            if key == "St":
                self.ts(dst, tab, self.cc(("sgn",)), None, ALU.mult, ALU.bypass, ["tab", "C"], [key])
            else:
                self.cp(dst, tab, ["tab"], [key])

    def rope(self, dst, pq, kpq, b0, n, t1, t2, wkeys, nope):
        Ct, St = self.Ct, self.St
        if nope:
            self.tt(dst[0:96, :], pq[0:96, 0:n], Ct[0:96, b0:b0 + n], ALU.mult, [kpq, "Ct"], [(wkeys[0], "n")])
            self.tt(t2[64:96, 0:n], pq[96:128, 0:n], St[96:128, b0:b0 + n], ALU.mult, [kpq, "St"], ["t2"])
            self.tt(dst[64:96, :], dst[64:96, :], t2[64:96, 0:n], ALU.add, ["t2", (wkeys[0], "n")], wkeys)
            return
        self.tt(t1[64:96, 0:n], pq[64:96, 0:n], Ct[64:96, b0:b0 + n], ALU.mult, [kpq, "Ct"], ["t1"])
        self.tt(t2[64:96, 0:n], pq[96:128, 0:n], St[96:128, b0:b0 + n], ALU.mult, [kpq, "St"], ["t2"])
        self.tt(dst[64:96, :], t1[64:96, 0:n], t2[64:96, 0:n], ALU.add, ["t1", "t2"], wkeys)

    def layer_c(self, j, nxt):
        P = self.P
        Ct, St = self.Ct, self.St
        hnk = lambda bi: [("hn", c, bi) for c in range(8)]
        P.fence()
        self.areset()
        cqn = self.abf(4 * T).rearrange("p (c t) -> p c t", c=4)
        ckvn = self.abf(2 * T).rearrange("p (c t) -> p c t", c=2)
        krt = self.abf(T)
        m0 = self.apos
        sq = self.abf(4 * 512).rearrange("p (c t) -> p c t", c=4)
        self.apos = m0
        pbuf = [self.abf(512) for _ in range(8)]
        rden = self.af32(512)
        t1 = self.af32(512)
        t2 = self.af32(512)
        t3 = self.af32(512)
        wkr, kkr = self.wload(("c", j, "kr"))
        for name, nt, dst, gk, D in (("cq", 4, cqn, ("qn", j), 512.0), ("ckv", 2, ckvn, ("kvn", j), 256.0)):
            wt = [self.wload(("c", j, name, t)) for t in range(nt)]
            for bi, (b0, n) in enumerate(BLKS):
                pbs = [self.bank() for _ in range(nt)]
                for t in range(nt):
                    self.lin(pbs[t][0], pbs[t][1], 128, wt[t][0], wt[t][1], 128, 0, self.hn, hnk(bi), b0, n)
                    self.act(sq[:, t, 0:n], pbs[t][0][:, 0:n], AF.Square, [pbs[t][1]], [("sq", t)])
                pss, kss = self.bank()
                for t in range(nt):
                    self.mm(pss[:, 0:n], self.ones, sq[:, t, 0:n], t == 0, t == nt - 1, ["cbf1", ("sq", t)], [kss])
                self.act(t1[:, 0:n], pss[:, 0:n], AF.Sqrt, [kss, "epsc"], ["t1"], scale=1.0 / D, bias=self.epsc)
                self.recip(t2[:, 0:n], t1[:, 0:n], ["t1"], ["t2"])
                for t in range(nt):
                    self.stt(dst[:, t, b0:b0 + n], pbs[t][0][:, 0:n], self.cc(gk, t), t2[:, 0:n], ALU.mult, ALU.mult,
                             [pbs[t][1], "t2", "C"], [(name, t, bi)])
        for bi, (b0, n) in enumerate(BLKS):
            pa, ka = self.bank()
            self.lin(pa, ka, 128, wkr, kkr, 128, 0, self.hn, hnk(bi), b0, n)
            self.rope(krt[:, b0:b0 + n], pa, ka, b0, n, t1, t2, [("krt", bi)], nope=False)
        for g in range(8):
            wg, kg_ = self.wload(("c", j, "gate", g))
            for bi, (b0, n) in enumerate(BLKS):
                pg, kpg = self.bank()
                self.lin(pg, kpg, 128, wg, kg_, 128, 0, self.hn, hnk(bi), b0, n)
                self.act(self.mix[:, g, b0:b0 + n], pg[:, 0:n], AF.Silu, [kpg], [("mix", g, bi)])
        P.fence()
        qh = [self.hn[:, 0, :], self.hn[:, 1, :]]
        kh = [self.hn[:, 2, :], self.hn[:, 3, :]]
        vh = [self.hn[:, 4:6, :].rearrange("p c t -> p (c t)")[:, 0:17 * 128].rearrange("p (t f) -> p t f", t=17),
              self.hn[:, 6:8, :].rearrange("p c t -> p (c t)")[:, 0:17 * 128].rearrange("p (t f) -> p t f", t=17)]
        for s in range(2):
            self.cp(kh[s][64:96, :], krt[64:96, :], [("krt", bi) for bi in range(5)], [("khr", s)])
            self.memset(vh[s][:, :, 64:128], 1.0, [("vh1", s)])
        cqk = lambda bi: [("cq", t, bi) for t in range(4)]
        ckk = lambda bi: [("ckv", t, bi) for t in range(2)]
        scale = 96.0 ** -0.5
        G_PO = ([6, 7], "po")
        G_ST = ([0, 1, 2, 3, 4, 5], "st")
        for hh in range(16):
            s = hh % 2
            wq, kq_ = self.wload(("c", j, "q", hh))
            if hh % 2 == 0:
                wkn, kkn = self.wload(("c", j, "kn2", hh // 2))
            wvv, kvv = self.wload(("c", j, "vv", hh))
            for bi, (b0, n) in enumerate(BLKS):
                pq, kpq = self.bank(G_ST)
                self.lin(pq, kpq, 128, wq, kq_, 128, 0, cqn, cqk(bi), b0, n, nk=4)
                self.rope(qh[s][:, b0:b0 + n], pq, kpq, b0, n, t1, t2, [("qh", s, bi)], nope=True)
                if hh % 2 == 0:
                    pkn, kpkn = self.bank(G_ST)
                    self.lin(pkn, kpkn, 128, wkn, kkn, 128, 0, ckvn, ckk(bi), b0, n, nk=2)
                    self.act(kh[0][0:64, b0:b0 + n], pkn[0:64, 0:n], AF.Copy, [kpkn], [("khn", 0, bi)])
                    self.act(kh[1][0:64, b0:b0 + n], pkn[64:128, 0:n], AF.Copy, [kpkn], [("khn", 1, bi)])
            for t0 in range(0, 17, 8):
                pv, kpv = self.bank(G_ST)
                tl = list(range(t0, min(17, t0 + 8)))
                for tt_ in tl:
                    tn = 128 if tt_ < 16 else 16
                    bi = min(tt_ // 4, 4)
                    for c in range(2):
                        self.mm(pv[0:tn, (tt_ - t0) * 64:(tt_ - t0) * 64 + 64], ckvn[:, c, tt_ * 128:tt_ * 128 + tn],
                                wvv[:, c * 64:(c + 1) * 64], c == 0, c == 1, [kvv, ("ckv", c, bi)], [kpv])
                if len(tl) == 8:
                    self.act(vh[s][:, t0:t0 + 8, 0:64], pv[:, 0:512].rearrange("p (t f) -> p t f", t=8), AF.Copy,
                             [kpv], [("vh", s, t0)])
                else:
                    self.act(vh[s][0:16, 16, 0:64], pv[0:16, 0:64], AF.Copy, [kpv], [("vh", s, t0)])
            vkeys = [("vh", s, 0), ("vh", s, 8), ("vh", s, 16), ("vh1", s)]
            for bi, (b0, n) in enumerate(BLKS):
                po, kpo = self.bank(G_PO)
                nkt = (b0 + n + 127) // 128
                kt_start = 0
                if n < 512:
                    pst, kst = self.bank(G_ST)
                    for kt in range(16):
                        self.mm(pst[:, kt * n:(kt + 1) * n], kh[s][0:96, kt * 128:kt * 128 + 128], qh[s][0:96, b0:b0 + n], True, True,
                                [("khr", s), ("khn", s, min(kt // 4, 4)), ("qh", s, bi), (("qh", s, bi), "n")], [kst])
                    pb_ = pbuf[7]
                    self.act(pb_[:, 0:16 * n], pst[:, 0:16 * n], AF.Exp, [kst], ["pb7"], scale=scale)
                    for kt in range(16):
                        self.mm(po[:, 0:n], vh[s][:, kt, :], pb_[:, kt * n:(kt + 1) * n], kt == 0, False, vkeys + ["pb7"], [kpo])
                    kt_start = 16
                for kt in range(kt_start, nkt):
                    kn_ = 128 if kt < 16 else 16
                    m = kt - b0 // 128
                    q0 = max(0, m) * 128 if n == 512 else 0
                    qn_ = n - q0
                    pst, kst = self.bank(G_ST)
                    diag = (m >= 0)
                    self.mm(pst[0:kn_, 0:qn_], kh[s][0:96, kt * 128:kt * 128 + kn_], qh[s][0:96, b0 + q0:b0 + n], True, not diag,
                            [("khr", s), ("khn", s, min(kt // 4, 4)), ("qh", s, bi), (("qh", s, bi), "n")], [kst])
                    if diag:
                        dn = min(128, qn_)
                        self.mm(pst[0:kn_, 0:dn], self.ident[0:kn_, 0:kn_], self.negtri[0:kn_, 0:dn], False, True, ["cbf"], [kst])
                    pi_ = kt % 8
                    pb_ = pbuf[pi_]
                    self.act(pb_[0:kn_, 0:qn_], pst[0:kn_, 0:qn_], AF.Exp, [kst], ["pb%d" % pi_], scale=scale)
                    self.mm(po[:, q0:n], vh[s][0:kn_, kt, :], pb_[0:kn_, 0:qn_], kt == 0, kt == nkt - 1,
                            vkeys + ["pb%d" % pi_], [kpo])
                self.recip(rden[64:128, 0:n], po[64:128, 0:n], [kpo], ["rden"])
                g = hh // 2
                r0 = (hh % 2) * 64
                self.tt(t3[r0:r0 + 64, 0:n], po[0:64, 0:n], rden[64:128, 0:n], ALU.mult, [kpo, "rden"], ["t3"])
                self.tt(self.mix[r0:r0 + 64, g, b0:b0 + n], t3[r0:r0 + 64, 0:n], self.mix[r0:r0 + 64, g, b0:b0 + n], ALU.mult,
                        ["t3", ("mix", g, bi)], [("mix", g, bi)])
        self.begin_layer(nxt)
        self.residual_bo(("c", "wo"), j, lambda c, bi: [("mix", c, bi)], post=lambda bi: self.norm_for(nxt, bi))

    def build(self):
        P = self.P
        self.prologue()
        self.areset()
        cs = self.af32(2)
        self.epsc = cs[:, 0:1]
        self.onec = cs[:, 1:2]
        self.memset(self.epsc, EPS, ["epsc"])
        self.memset(self.onec, 1.0, ["onec"])
        if self.nlayers > 1:
            self.rope_tables()
        self.abase = self.apos
        self.areset = lambda: setattr(self, "apos", self.abase)
        kinds = [("ab", l // 2) if l % 2 == 0 else ("c", l // 2) for l in range(self.nlayers)]
        self.okeys = []
        self.otc = 0
        self.begin_layer(kinds[0])
        for bi in range(len(BLKS)):
            self.norm_for(kinds[0], bi)
        for li, kind in enumerate(kinds):
            nxt = kinds[li + 1] if li + 1 < len(kinds) else None
            if kind[0] == "ab":
                self.layer_ab(kind[1], nxt)
            else:
                self.layer_c(kind[1], nxt)
        okeys = self.okeys
        if not self.final:
            for c in range(8):
                for bi, (b0, n) in enumerate(BLKS):
                    k = ("out", c, bi)
                    P.dma("sp", self.outT[c * 128:(c + 1) * 128, b0:b0 + n], self.h[:, c, b0:b0 + n],
                          reads=self.hkeys(c, bi), writes=[k])
                    okeys.append(k)
        P.wait_for("sp", okeys + self.dumpkeys)
        P.build()
        return self.nc


_CACHE = {}


def host_inputs(inputs):
    wl = weight_tiles(inputs)
    wts = np.ascontiguousarray(np.concatenate([a.astype(np.float32) for _, _, a in wl], axis=1))
    cl = const_cols(inputs)
    cst = np.ascontiguousarray(np.concatenate([a.astype(np.float32) for _, _, a in cl], axis=1))
    x = np.asarray(inputs["x"], np.float32)
    pos = np.asarray(inputs["positions"], np.int32)
    metaT = np.ascontiguousarray(np.asarray(inputs["meta_tokens"], np.float32).T)
    maps = []
    for b in range(x.shape[0]):
        maps.append({
            "xT": np.ascontiguousarray(x[b].T),
            "metaT": metaT,
            "posr": np.ascontiguousarray(np.broadcast_to(pos[b][None, :], (128, SEQ))),
            "wts": wts,
            "cst": cst,
        })
    return maps


def kernel(**inputs):
    inputs = {k: np.asarray(v) for k, v in inputs.items()}
    if "nc" not in _CACHE:
        _CACHE["nc"] = Builder(4, True).build()
    nc = _CACHE["nc"]
    maps = host_inputs(inputs)
    res = run_bass_kernel_spmd(nc, maps, core_ids=list(range(8)))
    out = np.stack([np.ascontiguousarray(r["outT"].T) for r in res.results], axis=0)
    return out.astype(np.float32)
```

```python
import math
import numpy as np
from contextlib import ExitStack
import concourse.bass as bass
import concourse.mybir as mybir
from concourse.bass_utils import run_bass_kernel_spmd

F32 = mybir.dt.float32
BF16 = mybir.dt.bfloat16
I32 = mybir.dt.int32
AF = mybir.ActivationFunctionType
ALU = mybir.AluOpType

EPOCH = 12000
T = 2064
NMETA = 16
SEQ = 2048
BLKS = [(0, 512), (512, 512), (1024, 512), (1536, 512), (2048, 16)]
EPS = 1e-6
NW = 8
NA = 11400
DBG = {}


ACT_SETS = [(0, ("Exp", "Tanh")), (2, ("Sigmoid", "Tanh")), (3, ("Sqrt",)), (5, ("Ln",)), (9, ("Sin",)),
            (18, ("Silu", "Tanh", "Sin"))]
ACT_FREE = ("Copy", "Identity", "Square")
SEM_LAT = 700.0
WINDOW = {"pe": 48, "act": 12, "dve": 12, "pool": 1, "sp": 1}


class Prog:
    QUEUES = ("sp", "act", "pool")

    def __init__(self, nc):
        self.nc = nc
        self.ops = {e: [] for e in ("pe", "act", "dve", "pool", "sp")}
        self.chan = {}
        self.lastw = {}
        self.readers = {}
        self.nch = {"sp": 8, "act": 2, "pool": 8}
        self.rr = {q: 0 for q in self.QUEUES}
        self.stack = ExitStack()
        self.cur_fence = {e: None for e in self.ops}
        self.nrec = 0
        self.since_fence = []
        self.do_sched = True

    def sb(self, name, shape, dt):
        return self.stack.enter_context(self.nc.sbuf_tensor(name, list(shape), dt))

    def ps(self, name, shape, dt=F32):
        return self.stack.enter_context(self.nc.psum_tensor(name, list(shape), dt))

    def _emit(self, eng, fn, reads, writes, is_dma=False, extra=(), cost=100.0, func=None, chan=None):
        deps = {}

        def add(r):
            if r is not None:
                deps[r["id"]] = r

        for k in reads:
            add(self.lastw.get(k))
        for k in writes:
            add(self.lastw.get(k))
            for r in self.readers.get(k, ()):
                add(r)
        for r in extra:
            add(r)
        add(self.cur_fence[eng])
        if is_dma:
            cl = self.chan.setdefault(chan, [])
            if cl:
                add(cl[-1])
        rec = {"id": self.nrec, "fn": fn, "eng": eng, "deps": list(deps.values()), "dma": is_dma, "chan": chan,
               "cost": cost, "func": func, "needed": False, "tag": getattr(self, "tag", None)}
        self.nrec += 1
        if is_dma:
            cl.append(rec)
        self.ops[eng].append(rec)
        self.since_fence.append(rec)
        for k in reads:
            self.readers.setdefault(k, []).append(rec)
        for k in writes:
            self.lastw[k] = rec
            self.readers[k] = []
        return rec

    def op(self, eng, fn, reads=(), writes=(), cost=100.0, func=None):
        return self._emit(eng, fn, list(reads), list(writes), cost=cost, func=func)

    def dma(self, q, out, in_, reads=(), writes=(), nbytes=65536):
        c = self.rr[q]
        self.rr[q] = (c + 1) % self.nch[q]
        return self._emit(q, lambda e: e.dma_start(out=out, in_=in_), list(reads), list(writes), is_dma=True,
                          chan=(q, c), cost=2000.0 + nbytes / 150.0)

    def wait_for(self, eng, keys):
        return self._emit(eng, None, list(keys), [], cost=0.0)

    def fence(self):
        allprev = [r for r in self.since_fence if r["fn"] is not None]
        for e in ("act", "dve"):
            r = self._emit(e, None, [], [], extra=allprev, cost=0.0)
            self.cur_fence[e] = r
        self.since_fence = []

    def schedule(self):
        fin = {}
        self.fence_times = []
        order = {e: [] for e in self.ops}
        pend = {e: list(l) for e, l in self.ops.items()}
        tfree = {e: 0.0 for e in self.ops}
        cur_set = [None]

        def act_switch(func):
            if func is None or func in ACT_FREE:
                return None
            if cur_set[0] is not None and func in cur_set[0]:
                return None
            for _, fs in ACT_SETS:
                if func in fs:
                    return fs
            return None

        def candidate(e):
            best = None
            lst = pend[e]
            W = WINDOW[e] if self.do_sched else 1
            seen_fence = False
            for pos in range(min(W, len(lst))):
                r = lst[pos]
                if r["fn"] is None and pos > 0:
                    break
                ok = True
                rdy = 0.0
                for d in r["deps"]:
                    f = fin.get(d["id"])
                    if f is None:
                        ok = False
                        break
                    lat = 0.0 if (d["eng"] == e and e == "pe" and not d["dma"]) else (SEM_LAT if d["eng"] != e or d["dma"] else 250.0)
                    if d["eng"] == e and not d["dma"] and e == "pe":
                        f = f - 150.0 + 20.0
                    if f + lat > rdy:
                        rdy = f + lat
                if ok:
                    st = max(tfree[e], rdy)
                    r["_rdy"] = rdy
                    pen = 1300.0 if (e == "act" and act_switch(r["func"]) is not None) else 0.0
                    key = (st + pen, r["id"])
                    if best is None or key < best[0]:
                        best = (key, pos, st, pen)
                if r["fn"] is None:
                    break
            return best

        remaining = sum(len(l) for l in pend.values())
        cands = {e: candidate(e) for e in self.ops}
        while remaining:
            be = None
            for e, c in cands.items():
                if c is not None and (be is None or c[0] < cands[be][0]):
                    be = e
            assert be is not None, "scheduler deadlock"
            key, pos, st, pen = cands[be]
            r = pend[be].pop(pos)
            r["_st"] = st
            r["_prev"] = order[be][-1] if order[be] else None
            r["_engwait"] = tfree[be] >= r.get("_rdy", 0.0)
            if be == "act":
                fs = act_switch(r["func"])
                if fs is not None:
                    cur_set[0] = fs
            if r["dma"]:
                tfree[be] = st + (900.0 if be == "pool" else 100.0)
                fin[r["id"]] = st + r["cost"]
            else:
                tfree[be] = st + pen + r["cost"]
                fin[r["id"]] = tfree[be] if be != "pe" else tfree[be] + 150.0
            r["_end"] = fin[r["id"]]
            order[be].append(r)
            if r["fn"] is None and be == "act" and len(r["deps"]) > 20:
                self.fence_times.append(st)
            remaining -= 1
            for e in self.ops:
                cands[e] = candidate(e)
        self.sim_time = max(fin.values()) if fin else 0.0
        return order

    def build(self):
        nc = self.nc
        order = self.schedule()
        vlist = {}
        for e, lst in order.items():
            for r in lst:
                v = ("dma",) + r["chan"] if r["dma"] else e
                l = vlist.setdefault(v, [])
                r["veng"] = v
                r["vpos"] = len(l)
                l.append(r)
        for e, lst in order.items():
            known = {}
            for r in lst:
                need = {}
                for d in r["deps"]:
                    v = d["veng"]
                    if v == "pe" and e == "pe" and not r["dma"]:
                        continue
                    if d["fn"] is None:
                        continue
                    if need.get(v, -1) < d["vpos"]:
                        need[v] = d["vpos"]
                waits = []
                for v, p in need.items():
                    if known.get(v, -1) >= p:
                        continue
                    known[v] = p
                    waits.append(vlist[v][p])
                r["waits"] = waits
                for w in waits:
                    w["needed"] = True
        sems = {}
        for v, lst in vlist.items():
            cnt = 0
            for r in lst:
                if r["fn"] is not None and (r["dma"] or r["needed"]):
                    r["sval"] = cnt
                    cnt += 1
                else:
                    r["sval"] = None
            nep = max(1, (cnt + EPOCH - 1) // EPOCH)
            nm = "_".join(map(str, v)) if isinstance(v, tuple) else v
            sems[v] = [self.stack.enter_context(nc.semaphore("s_%s_%d" % (nm, k))) for k in range(nep)]

        def replay(ename, e):
            for r in order[ename]:
                for t in r["waits"]:
                    sv = t["sval"]
                    mult = 16 if t["dma"] else 1
                    e.wait_ge(sems[t["veng"]][sv // EPOCH], (sv % EPOCH + 1) * mult)
                if r["fn"] is None:
                    continue
                ins = r["fn"](e)
                if r["sval"] is not None:
                    sv = r["sval"]
                    ins.then_inc(sems[r["veng"]][sv // EPOCH], 16 if r["dma"] else 1)

        with nc.Block() as block:
            @block.sync
            def _(e):
                replay("sp", e)

            @block.tensor
            def _(e):
                replay("pe", e)

            @block.scalar
            def _(e):
                replay("act", e)

            @block.vector
            def _(e):
                replay("dve", e)

            @block.gpsimd
            def _(e):
                replay("pool", e)
        self.stack.close()


def pcn(w):
    K, n = w.shape
    return np.ascontiguousarray(w.reshape(K // 128, 128, n).transpose(1, 0, 2).reshape(128, -1))


def weight_tiles(inp=None):
    out = []

    def add(key, ncols, fn):
        out.append((key, ncols, fn() if inp is not None else None))

    def pad128(rows):
        a = np.zeros((128, rows.shape[1]), np.float32)
        a[: rows.shape[0]] = rows
        return a

    for layer in range(4):
        j = layer // 2
        if layer % 2 == 0:
            wi = inp["ab_w_in"][j] if inp is not None else None
            wo = inp["ab_w_out"][j] if inp is not None else None
            for g in range(8):
                add(("ab", j, "xa", g), 1024, lambda: pcn(wi[:, 128 * g:128 * g + 128]))
                add(("ab", j, "ga", g), 1024, lambda: pcn(wi[:, 1024 + 128 * g:1024 + 128 * g + 128]))
                add(("ab", j, "gaw", g), 128, lambda: np.ascontiguousarray(inp["ab_gate_a_w"][j, g]))
                add(("ab", j, "gxw", g), 128, lambda: np.ascontiguousarray(inp["ab_gate_x_w"][j, g]))
            for d in range(8):
                add(("ab", j, "woA", d), 1024, lambda: pcn(wo[0:1024, 128 * d:128 * d + 128]))
            add(("ab", j, "ad"), 1024, lambda: pcn(np.concatenate([wi[:, 4096:4112], np.zeros((1024, 112), np.float32)], axis=1)))
            add(("ab", j, "alw"), 512, lambda: pad128(inp["ab_alpha_w"][j]))
            for hd in range(4):
                add(("ab", j, "q", hd), 1024, lambda: pcn(wi[:, 2048 + 128 * hd:2048 + 128 * hd + 128]))
                add(("ab", j, "k", hd), 1024, lambda: pcn(wi[:, 2560 + 128 * hd:2560 + 128 * hd + 128]))
            for c in range(8):
                add(("ab", j, "v", c), 1024, lambda: np.ascontiguousarray(wi[128 * c:128 * c + 128, 3072:4096]))
            for g in range(8):
                add(("ab", j, "gb", g), 1024, lambda: pcn(wi[:, 4112 + 128 * g:4112 + 128 * g + 128]))
            for d in range(8):
                add(("ab", j, "woB", d), 1024, lambda: pcn(wo[1024:2048, 128 * d:128 * d + 128]))
        else:
            wi = inp["c_w_in"][j] if inp is not None else None
            wq = inp["c_w_q_up"][j] if inp is not None else None
            wkv = inp["c_w_kv_up"][j] if inp is not None else None
            wo = inp["c_w_out"][j] if inp is not None else None

            def ropepad(src, c0, swap, nope0=None):
                a = np.zeros((src.shape[0], 128), np.float32)
                if nope0 is not None:
                    a[:, 0:64] = src[:, nope0:nope0 + 64]
                x1 = src[:, c0:c0 + 16]
                x2 = src[:, c0 + 16:c0 + 32]
                a[:, 64:80] = x1
                a[:, 80:96] = x2
                a[:, 96:112] = x2
                a[:, 112:128] = x1
                return a

            for t in range(4):
                add(("c", j, "cq", t), 1024, lambda: pcn(wi[:, 128 * t:128 * t + 128]))
            for t in range(2):
                add(("c", j, "ckv", t), 1024, lambda: pcn(wi[:, 512 + 128 * t:512 + 128 * t + 128]))
            add(("c", j, "kr"), 1024, lambda: pcn(ropepad(wi, 768, False)))
            for g in range(8):
                add(("c", j, "gate", g), 1024, lambda: pcn(wi[:, 800 + 128 * g:800 + 128 * g + 128]))
            for hh in range(16):
                add(("c", j, "q", hh), 512, lambda: pcn(ropepad(wq, 96 * hh + 64, False, nope0=96 * hh)))
                if hh % 2 == 0:
                    add(("c", j, "kn2", hh // 2), 256, lambda: pcn(np.concatenate(
                        [wkv[:, 128 * hh:128 * hh + 64], wkv[:, 128 * (hh + 1):128 * (hh + 1) + 64]], axis=1)))
                add(("c", j, "vv", hh), 128, lambda: pcn(wkv[:, 128 * hh + 64:128 * hh + 128]))
            for d in range(8):
                add(("c", j, "wo", d), 1024, lambda: pcn(wo[:, 128 * d:128 * d + 128]))
    return out


def col(v):
    return np.ascontiguousarray(np.asarray(v, np.float32).reshape(-1, 128).T)


def const_cols(inp=None):
    out = []

    def add(key, ncols, fn):
        out.append((key, ncols, fn() if inp is not None else None))

    for j in range(2):
        add(("abn", j), 8, lambda: col(inp["ab_norm"][j]))
        add(("convw", j), 32, lambda: np.ascontiguousarray(
            inp["ab_conv_w"][j].reshape(4, 8, 128).transpose(2, 1, 0).reshape(128, 32)))
        add(("convb", j), 8, lambda: col(inp["ab_conv_b"][j]))
        add(("gab", j), 8, lambda: col(inp["ab_gate_a_b"][j]))
        add(("gxb", j), 8, lambda: col(inp["ab_gate_x_b"][j]))
        add(("lam", j), 8, lambda: col(inp["ab_lru_lambda"][j]))
        add(("alb", j), 4, lambda: col(inp["ab_alpha_b"][j]))
        add(("gln", j), 2, lambda: col(inp["ab_gla_norm"][j]))
        add(("cn", j), 8, lambda: col(inp["c_norm"][j]))
        add(("qn", j), 4, lambda: col(inp["c_q_norm"][j]))
        add(("kvn", j), 2, lambda: col(inp["c_kv_norm"][j]))
    add(("fn",), 8, lambda: col(inp["final_norm"]))

    def invf():
        a = np.zeros((128, 1), np.float32)
        f = (10000.0 ** (-np.arange(0, 32, 2, dtype=np.float32) / 32.0)).astype(np.float32)
        for r0 in (64, 80, 96, 112):
            a[r0:r0 + 16, 0] = f
        return a

    def sgn():
        a = np.zeros((128, 1), np.float32)
        a[96:112, 0] = -1.0
        a[112:128, 0] = 1.0
        return a

    add(("invf",), 1, invf)
    add(("sgn",), 1, sgn)
    add(("metapos",), 16, lambda: np.tile(np.arange(16, dtype=np.float32)[None], (128, 1)))
    add(("tri",), 128, lambda: np.triu(np.ones((128, 128), np.float32)))
    add(("ident",), 128, lambda: np.eye(128, dtype=np.float32))
    add(("negtri",), 128, lambda: np.tril(np.full((128, 128), -30000.0, np.float32), k=-1))

    def rmask():
        a = np.ones((128, 512), np.float32)
        a[:, ::128] = 0.0
        return a

    add(("rmask",), 512, rmask)
    return out


def offsets(lst):
    off = {}
    o = 0
    for key, n, _ in lst:
        off[key] = (o, n)
        o += n
    return off, o


class Builder:
    def __init__(self, nlayers=4, final=True):
        self.nlayers = nlayers
        self.final = final
        nc = bass.Bass("TRN2", target_bir_lowering=False)
        self.nc = nc
        P = self.P = Prog(nc)
        self.woff, wtot = offsets(weight_tiles())
        self.coff, ctot = offsets(const_cols())
        self.xT = nc.dram_tensor("xT", [1024, SEQ], F32, kind="ExternalInput").ap()
        self.metaT = nc.dram_tensor("metaT", [1024, NMETA], F32, kind="ExternalInput").ap()
        self.posr = nc.dram_tensor("posr", [128, SEQ], I32, kind="ExternalInput").ap()
        self.wts = nc.dram_tensor("wts", [128, wtot], F32, kind="ExternalInput").ap()
        self.cst = nc.dram_tensor("cst", [128, ctot], F32, kind="ExternalInput").ap()
        self.outT = nc.dram_tensor("outT", [1024, SEQ if final else T], F32, kind="ExternalOutput").ap()

        self.h = P.sb("h", [128, 8 * T], F32)[:, :].rearrange("p (c t) -> p c t", c=8)
        self.hn = P.sb("hn", [128, 8 * T], BF16)[:, :].rearrange("p (c t) -> p c t", c=8)
        self.mix2d = P.sb("mix", [128, 8 * T], BF16)[:, :]
        self.mix = self.mix2d.rearrange("p (c t) -> p c t", c=8)
        self.Ct = P.sb("Ct", [128, T], BF16)[:, :]
        self.St = P.sb("St", [128, T], BF16)[:, :]
        self.C = P.sb("cst_sb", [128, ctot], F32)
        self.cbf = P.sb("cbf", [128, 512], BF16)
        self.wp = [P.sb("wp%d" % i, [128, 1024], BF16) for i in range(NW)]
        self.wi = 0
        self.arena = P.sb("arena", [128, NA], F32)
        self.banks = [P.ps("ps%d" % i, [128, 512], F32) for i in range(8)]
        self.bi = 0
        self.gctr = {}
        self.dumpkeys = []
        self.apos = 0

    def cc(self, key, c0=0, n=1, rows=slice(0, 128)):
        o, _ = self.coff[key]
        return self.C[rows, o + c0:o + c0 + n]

    def bank(self, grp=None):
        if grp is None:
            i = self.bi
            self.bi = (i + 1) % 8
        else:
            lst, name = grp
            k = self.gctr.get(name, 0)
            self.gctr[name] = k + 1
            i = lst[k % len(lst)]
        return self.banks[i], "ps%d" % i

    def areset(self):
        self.apos = 0

    def af32(self, n):
        v = self.arena[:, self.apos:self.apos + n]
        self.apos += n
        assert self.apos <= NA, self.apos
        return v

    def abf(self, n):
        assert n % 2 == 0
        return self.af32(n // 2).bitcast(BF16)

    def wload(self, key):
        o, n = self.woff[key]
        i = self.wi
        self.wi = (i + 1) % NW
        buf = self.wp[i]
        self.P.dma("pool", buf[:, 0:n], self.wts[:, o:o + n], writes=["wp%d" % i], nbytes=n * 512)
        return buf, "wp%d" % i

    @staticmethod
    def fsz(ap):
        n = 1
        for d in ap.shape[1:]:
            n *= d
        return n

    def mm(self, out, lhsT, rhs, start, stop, reads, writes):
        n = self.fsz(rhs)
        self.P.op("pe", lambda e: e.matmul(out, lhsT=lhsT, rhs=rhs, start=start, stop=stop), reads, writes,
                  cost=max(64, n) / 1.95 + 12.0)

    def act(self, out, in_, func, reads, writes, scale=1.0, bias=0.0):
        self.P.op("act", lambda e: e.activation(out=out, in_=in_, func=func, bias=bias, scale=scale), reads, writes,
                  cost=185.0 + self.fsz(out) / 1.2, func=func.name)

    def dcost(self, out, f):
        return 65.0 + f * self.fsz(out) / 0.96

    def tt(self, out, in0, in1, op, reads, writes, eng="dve"):
        self.P.op(eng, lambda e: e.tensor_tensor(out=out, in0=in0, in1=in1, op=op), reads, writes,
                  cost=self.dcost(out, 1.0 if eng == "dve" else 2.0))

    def ts(self, out, in0, s1, s2, op0, op1, reads, writes, eng="dve"):
        self.P.op(eng, lambda e: e.tensor_scalar(out=out, in0=in0, scalar1=s1, scalar2=s2, op0=op0, op1=op1), reads, writes,
                  cost=self.dcost(out, 0.6))

    def stt(self, out, in0, scalar, in1, op0, op1, reads, writes):
        self.P.op("dve", lambda e: e.scalar_tensor_tensor(out=out, in0=in0, scalar=scalar, in1=in1, op0=op0, op1=op1), reads, writes,
                  cost=self.dcost(out, 1.0))

    def cp(self, out, in_, reads, writes, eng="dve"):
        self.P.op(eng, lambda e: e.tensor_copy(out=out, in_=in_), reads, writes, cost=self.dcost(out, 0.6))

    def recip(self, out, in_, reads, writes):
        self.P.op("dve", lambda e: e.reciprocal(out=out, in_=in_), reads, writes, cost=self.dcost(out, 1.0))

    def scan(self, out, d0, d1, init, reads, writes):
        self.P.op("dve", lambda e: e.tensor_tensor_scan(out=out, data0=d0, data1=d1, initial=init, op0=ALU.mult, op1=ALU.add), reads, writes,
                  cost=self.dcost(out, 2.0))

    def memset(self, ap, val, writes, eng="dve"):
        self.P.op(eng, lambda e: e.memset(ap, val), [], writes, cost=self.dcost(ap, 0.5))

    def dump(self, name, ap, keys):
        shp = list(ap.shape)
        d = self.nc.dram_tensor("dbg_" + name, shp, ap.dtype, kind="ExternalOutput").ap()
        k = ("dbg", name)
        self.P.dma("sp", d, ap, reads=keys, writes=[k])
        self.dumpkeys.append(k)

    def lin(self, pb, pk, M, wbuf, wkey, ncol, c0, src, srckeys, b0, n, nk=8, m0=0):
        for c in range(nk):
            self.mm(pb[m0:m0 + M, 0:n], wbuf[:, c * ncol + c0:c * ncol + c0 + M], src[:, c, b0:b0 + n],
                    c == 0, c == nk - 1, [wkey] + srckeys, [pk])

    def prologue(self):
        P = self.P
        P.dma("sp", self.C[:, :], self.cst[:, :], writes=["C"])
        for c in range(8):
            P.dma("sp", self.h[:, c, 0:NMETA], self.metaT[c * 128:(c + 1) * 128, :], writes=[("h", c, 0)], nbytes=8192)
        for bi, (b0, n) in enumerate(BLKS):
            for c in range(8):
                lo = max(b0, NMETA)
                P.dma("sp", self.h[:, c, lo:b0 + n], self.xT[c * 128:(c + 1) * 128, lo - NMETA:b0 + n - NMETA],
                      writes=[("h", c, bi)] if bi > 0 else [("h", c, 0, "x")], nbytes=(b0 + n - lo) * 512)
        o, _ = self.coff[("tri",)]
        self.cp(self.cbf[:, 0:384], self.C[:, o:o + 384], ["C"], ["cbf"])
        self.memset(self.cbf[:, 384:512], 1.0, ["cbf1"])
        self.tri = self.cbf[:, 0:128]
        self.ident = self.cbf[:, 128:256]
        self.negtri = self.cbf[:, 256:384]
        self.ones = self.cbf[:, 384:512]

    def hkeys(self, c, bi):
        return [("h", c, bi)] + ([("h", c, 0, "x")] if bi == 0 else [])

    def rmsnorm_h(self, gkey, dst_fn):
        for bi in range(len(BLKS)):
            self.rmsnorm_block(gkey, dst_fn, bi)

    def rmsnorm_block(self, gkey, dst_fn, bi):
        if True:
            b0, n = BLKS[bi]
            pb, pk = self.bank(([6, 7], "norm"))
            d = bi % 2
            sq, nsd, nrs = self.nsq[d], self.nsd[d], self.nrs[d]
            for c in range(8):
                self.act(sq[:, c, 0:n], self.h[:, c, b0:b0 + n], AF.Square, self.hkeys(c, bi), [("nsq", d, c)])
            for c in range(8):
                self.mm(pb[:, 0:n], self.ones, sq[:, c, 0:n], c == 0, c == 7, ["cbf1", ("nsq", d, c)], [pk])
            self.act(nsd[:, 0:n], pb[:, 0:n], AF.Sqrt, [pk, "epsc"], [("nsd", d)], scale=1.0 / 1024.0, bias=self.epsc)
            self.recip(nrs[:, 0:n], nsd[:, 0:n], [("nsd", d)], [("nrs", d)])
            for c in range(8):
                o, w = dst_fn(c, bi, b0, n)
                self.stt(o, self.h[:, c, b0:b0 + n], self.cc(gkey, c), nrs[:, 0:n], ALU.mult, ALU.mult,
                         self.hkeys(c, bi) + [("nrs", d), "C"], w)

    def norm_bufs(self):
        self.nsq = [self.abf(8 * 512).rearrange("p (c t) -> p c t", c=8) for _ in range(2)]
        self.nsd = [self.af32(512) for _ in range(2)]
        self.nrs = [self.af32(512) for _ in range(2)]

    def residual(self, wkind, j, srckeyfn):
        for d in range(8):
            wb, wk = self.wload((wkind[0], j, wkind[1], d))
            for bi, (b0, n) in enumerate(BLKS):
                pb, pk = self.bank()
                for c in range(8):
                    self.mm(pb[:, 0:n], wb[:, c * 128:(c + 1) * 128], self.mix[:, c, b0:b0 + n], c == 0, c == 7,
                            [wk] + srckeyfn(c, bi), [pk])
                self.tt(self.h[:, d, b0:b0 + n], self.h[:, d, b0:b0 + n], pb[:, 0:n], ALU.add,
                        self.hkeys(d, bi) + [pk], [("h", d, bi)] + ([("h", d, 0, "x")] if bi == 0 else []))

    def residual_bo(self, wkind, j, srckeyfn, post=None):
        wt = [self.wload((wkind[0], j, wkind[1], d)) for d in range(8)]
        for bi, (b0, n) in enumerate(BLKS):
            for d in range(8):
                wb, wk = wt[d]
                pb, pk = self.bank(([0, 1, 2, 3, 4, 5], "res"))
                for c in range(8):
                    self.mm(pb[:, 0:n], wb[:, c * 128:(c + 1) * 128], self.mix[:, c, b0:b0 + n], c == 0, c == 7,
                            [wk] + srckeyfn(c, bi), [pk])
                self.tt(self.h[:, d, b0:b0 + n], self.h[:, d, b0:b0 + n], pb[:, 0:n], ALU.add,
                        self.hkeys(d, bi) + [pk], [("h", d, bi)] + ([("h", d, 0, "x")] if bi == 0 else []))
            if post is not None:
                post(bi)

    def norm_for(self, kind, bi):
        if kind is None:
            if self.final:
                self.final_block(bi)
            return
        gkey = ("abn", kind[1]) if kind[0] == "ab" else ("cn", kind[1])
        self.rmsnorm_block(gkey, lambda c, bi_, b0, n: (self.hn[:, c, b0:b0 + n], [("hn", c, bi_)]), bi)

    def final_block(self, bi):
        P = self.P

        def dst(c, bi_, b0, n):
            i = self.otc % 4
            self.otc += 1
            self._cur = (c, bi_, b0, n)
            return self.ot[i][:, 0:n], ["ot%d" % i]

        orig_stt = self.stt

        def stt2(out, in0, scalar, in1, op0, op1, reads, writes):
            orig_stt(out, in0, scalar, in1, op0, op1, reads, writes)
            c, bi_, b0, n = self._cur
            lo = max(b0, NMETA)
            k = ("out", c, bi_)
            P.dma("sp", self.outT[c * 128:(c + 1) * 128, lo - NMETA:b0 + n - NMETA], out[:, lo - b0:n], reads=writes, writes=[k])
            self.okeys.append(k)

        self.stt = stt2
        self.rmsnorm_block(("fn",), dst, bi)
        self.stt = orig_stt

    def begin_layer(self, nxt):
        self.P.fence()
        self.areset()
        if nxt is not None and nxt[0] == "ab":
            self.ab_consts(nxt[1])
        self.norm_bufs()
        if nxt is None:
            self.ot = [self.af32(512) for _ in range(4)]

    def ab_consts(self, j):
        cpc = self.af32(8)
        cp2 = self.af32(8)
        nalb = self.af32(4)
        self.act(cpc, self.cc(("lam", j), 0, 8), AF.Exp, ["C"], ["cpc"], scale=-1.0)
        self.act(cpc, cpc, AF.Ln, ["cpc", "onec"], ["cpc"], bias=self.onec)
        self.ts(cp2, cpc, -16.0, None, ALU.mult, ALU.bypass, ["cpc"], ["cp2"])
        self.ts(cpc, cpc, -8.0, None, ALU.mult, ALU.bypass, ["cpc"], ["cpc"])
        self.ts(nalb, self.cc(("alb", j), 0, 4), -1.0, None, ALU.mult, ALU.bypass, ["C"], ["nalb"])
        self.abc = (cpc, cp2, nalb)
        self.amark = self.apos

    def layer_ab(self, j, nxt_layer):
        P = self.P
        cpc, cp2, nalb = self.abc
        amark = self.amark
        hnk = lambda bi: [("hn", c, bi) for c in range(8)]
        P.fence()
        self.apos = amark
        HN = 1040
        xb = [self.abf(4 + 512) for _ in range(2)]
        dg = self.abf(4 * 128).rearrange("p (k m) -> p k m", k=4)
        cvb = [self.abf(512) for _ in range(2)]
        tr = [self.af32(512) for _ in range(2)]
        ti = [self.af32(512) for _ in range(2)]
        tg = self.af32(512)
        Ah = [self.af32(HN) for _ in range(2)]
        Uh = [self.af32(HN) for _ in range(2)]
        Wh = [self.abf(HN) for _ in range(2)]
        A2 = self.af32(HN)
        Hs = self.af32(HN)
        hl = self.af32(1)
        hc = self.af32(8)
        hba = self.af32(8)
        hbx = self.af32(8)
        self.ts(hc, cpc, 0.5, None, ALU.mult, ALU.bypass, ["cpc"], ["hc"])
        self.ts(hba, self.cc(("gab", j), 0, 8), 0.5, None, ALU.mult, ALU.bypass, ["C"], ["hba"])
        self.ts(hbx, self.cc(("gxb", j), 0, 8), 0.5, None, ALU.mult, ALU.bypass, ["C"], ["hbx"])
        o_id, _ = self.coff[("ident",)]
        cnt = 0
        hcnt = 0
        for g in range(8):
            wxa, kxa = self.wload(("ab", j, "xa", g))
            wga, kga = self.wload(("ab", j, "ga", g))
            wra, kra = self.wload(("ab", j, "gaw", g))
            wrx, krx = self.wload(("ab", j, "gxw", g))
            for k in range(4):
                self.ts(dg[:, k, :], self.C[:, o_id:o_id + 128], self.cc(("convw", j), g * 4 + k), None, ALU.mult, ALU.bypass,
                        ["C"], [("dg", k)])
            self.memset(xb[0][:, 0:3], 0.0, ["xb0h"])
            for hi, hblks in enumerate(((0, 1), (2, 3, 4))):
                hp = hcnt % 2
                hcnt += 1
                A_, U_, W_ = Ah[hp], Uh[hp], Wh[hp]
                kA, kU, kW = "Ah%d" % hp, "Uh%d" % hp, "Wh%d" % hp
                off = 0
                for bi in hblks:
                    b0, n = BLKS[bi]
                    cur, nxt = bi % 2, (bi + 1) % 2
                    d = cnt % 2
                    cnt += 1
                    px, kx = self.bank()
                    pg, kg = self.bank()
                    self.lin(px, kx, 128, wxa, kxa, 128, 0, self.hn, hnk(bi), b0, n)
                    self.lin(pg, kg, 128, wga, kga, 128, 0, self.hn, hnk(bi), b0, n)
                    X = xb[cur]
                    xk = "xb%d" % cur
                    self.cp(X[:, 3:3 + n], px[:, 0:n], [kx], [xk])
                    if bi + 1 < len(BLKS):
                        self.cp(xb[nxt][:, 0:3], X[:, n:n + 3], [xk], ["xb%dh" % nxt])
                    pc, kc = self.bank()
                    for k in range(4):
                        self.mm(pc[:, 0:n], dg[:, k, :], X[:, k:k + n], k == 0, k == 3, [("dg", k), xk, xk + "h"], [kc])
                    self.act(cvb[d][:, 0:n], pc[:, 0:n], AF.Identity, [kc, "C"], ["cvb%d" % d], bias=self.cc(("convb", j), g))
                    pr, kr = self.bank()
                    pi, ki = self.bank()
                    self.mm(pr[:, 0:n], wra[:, 0:128], cvb[d][:, 0:n], True, True, [kra, "cvb%d" % d], [kr])
                    self.mm(pi[:, 0:n], wrx[:, 0:128], cvb[d][:, 0:n], True, True, [krx, "cvb%d" % d], [ki])
                    self.act(tr[d][:, 0:n], pr[:, 0:n], AF.Tanh, [kr, "hba"], ["tr%d" % d], scale=0.5, bias=hba[:, g:g + 1])
                    self.act(ti[d][:, 0:n], pi[:, 0:n], AF.Tanh, [ki, "hbx"], ["ti%d" % d], scale=0.5, bias=hbx[:, g:g + 1])
                    self.act(A_[:, off:off + n], tr[d][:, 0:n], AF.Exp, ["tr%d" % d, "hc"], [(kA, bi)], scale=hc[:, g:g + 1], bias=hc[:, g:g + 1])
                    self.stt(U_[:, off:off + n], ti[d][:, 0:n], 1.0, cvb[d][:, 0:n], ALU.add, ALU.mult, ["ti%d" % d, "cvb%d" % d], [(kU, bi)])
                    self.act(tg[:, 0:n], pg[:, 0:n], AF.Tanh, [kg], ["tg"], scale=0.5)
                    self.stt(W_[:, off:off + n], tg[:, 0:n], 1.0, pg[:, 0:n], ALU.add, ALU.mult, ["tg", kg], [(kW, bi)])
                    off += n
                N = off
                t0 = BLKS[hblks[0]][0]
                akeys = [(kA, bi) for bi in hblks]
                ukeys = [(kU, bi) for bi in hblks]
                wkeys = [(kW, bi) for bi in hblks]
                self.act(A2[:, 0:N], A_[:, 0:N], AF.Square, akeys, ["A2"])
                self.act(A2[:, 0:N], A2[:, 0:N], AF.Sqrt, ["A2", "onec"], ["A2"], scale=-1.0, bias=self.onec)
                self.stt(U_[:, 0:N], U_[:, 0:N], 0.5, A2[:, 0:N], ALU.mult, ALU.mult, ukeys + ["A2"], ukeys)
                if hi == 0:
                    init, ird = 0.0, []
                else:
                    init, ird = hl[:, 0:1], ["hl"]
                self.scan(Hs[:, 0:N], A_[:, 0:N], U_[:, 0:N], init, akeys + ukeys + ird, ["Hs"])
                if hi == 0:
                    self.cp(hl[:, 0:1], Hs[:, N - 1:N], ["Hs"], ["hl"])
                self.stt(self.mix[:, g, t0:t0 + N], Hs[:, 0:N], 0.5, W_[:, 0:N], ALU.mult, ALU.mult, ["Hs"] + wkeys,
                         [("mix", g, bi) for bi in hblks])
        if not DBG.get("skipA"):
            self.residual(("ab", "woA"), j, lambda c, bi: [("mix", c, bi)])
        P.fence()
        self.apos = amark
        adT = self.abf(512)
        qd = [self.abf(512) for _ in range(2)]
        kinv = [self.abf(512) for _ in range(2)]
        kend = [self.abf(512) for _ in range(2)]
        vtok = self.abf(4 * 1024).rearrange("p (t f) -> p t f", t=4)
        sgbs = [self.abf(2 * 512).rearrange("p (g t) -> p g t", g=2) for _ in range(2)]
        obs = [self.af32(2 * 512).rearrange("p (g t) -> p g t", g=2) for _ in range(2)]
        S = self.af32(4 * 256).rearrange("p (h e) -> p h e", h=4)
        Sb = self.abf(4 * 256).rearrange("p (h e) -> p h e", h=4)
        ebl = self.af32(16).rearrange("p (h c) -> p h c", h=4)
        t1 = self.af32(512)
        t2 = self.af32(512)
        eb = self.af32(512)
        t4 = self.af32(512)
        stm = [self.abf(128) for _ in range(2)]
        ktk = [self.abf(128) for _ in range(2)]
        osqs = [self.abf(2 * 512).rearrange("p (e t) -> p e t", e=2)]
        self.memset(S[:, :, :], 0.0, [("S", hd) for hd in range(4)])
        self.memset(Sb[:, :, :], 0.0, [("Sb", hd) for hd in range(4)])
        for bi, (b0, n) in enumerate(BLKS):
            nch = (n + 127) // 128
            wad, kad = self.wload(("ab", j, "ad"))
            pa, ka = self.bank()
            self.lin(pa, ka, 128, wad, kad, 128, 0, self.hn, hnk(bi), b0, n)
            self.act(adT[0:16, 0:n], pa[0:16, 0:n], AF.Copy, [ka], ["adT"])
            P.tag = ("B", bi, -1, "v")
            pvs = {}
            for tt_ in range(nch):
                for hf in range(2):
                    pvs[(tt_, hf)] = self.bank()
            for c in range(8):
                wvc, kwv = self.wload(("ab", j, "v", c))
                for tt_ in range(nch):
                    tn = min(128, n - tt_ * 128)
                    for hf in range(2):
                        pv, kpv = pvs[(tt_, hf)]
                        self.mm(pv[0:tn, 0:512], self.hn[:, c, b0 + tt_ * 128:b0 + tt_ * 128 + tn],
                                wvc[:, hf * 512:(hf + 1) * 512], c == 0, c == 7, [kwv, ("hn", c, bi)], [kpv])
            for tt_ in range(nch):
                tn = min(128, n - tt_ * 128)
                for hf in range(2):
                    pv, kpv = pvs[(tt_, hf)]
                    self.act(vtok[0:tn, tt_, hf * 512:(hf + 1) * 512], pv[0:tn, 0:512], AF.Copy, [kpv], [("vtok", tt_, hf)])
            def st_pre(hd):
                par = hd % 2
                QD, KI, KE = qd[par], kinv[par], kend[par]
                kQD, kKI = "qd%d" % par, "kinv%d" % par
                ob, osq, kob = obs[par], osqs[0], "ob%d" % par
                sgb = sgbs[par]
                walw, kalw = self.wload(("ab", j, "alw"))
                wq, kq = self.wload(("ab", j, "q", hd))
                wk, kk = self.wload(("ab", j, "k", hd))
                pq, kpq = self.bank()
                pk, kpk = self.bank()
                pz, kpz = self.bank()
                self.lin(pq, kpq, 128, wq, kq, 128, 0, self.hn, hnk(bi), b0, n)
                self.lin(pk, kpk, 128, wk, kk, 128, 0, self.hn, hnk(bi), b0, n)
                self.mm(pz[:, 0:n], walw[0:16, hd * 128:(hd + 1) * 128], adT[0:16, 0:n], True, True, [kalw, "adT"], [kpz])
                self.act(t1[:, 0:n], pz[:, 0:n], AF.Exp, [kpz, "nalb"], ["t1"], scale=-1.0, bias=nalb[:, hd:hd + 1])
                self.act(t1[:, 0:n], t1[:, 0:n], AF.Ln, ["t1", "onec"], ["t1"], bias=self.onec)
                self.scan(t2[:, 0:n], self.cc(("rmask",), 0, n), t1[:, 0:n], 0.0, ["t1", "C"], ["t2"])
                self.act(eb[:, 0:n], t2[:, 0:n], AF.Exp, ["t2"], ["eb"], scale=-1.0 / 16.0)
                self.act(t1[:, 0:n], t2[:, 0:n], AF.Exp, ["t2"], ["t1"], scale=1.0 / 16.0)
                self.stt(QD[:, 0:n], pq[:, 0:n], 128.0 ** -0.5, eb[:, 0:n], ALU.mult, ALU.mult, [kpq, "eb"], [kQD])
                self.tt(KI[:, 0:n], pk[:, 0:n], t1[:, 0:n], ALU.mult, [kpk, "t1"], [kKI])
                for ci in range(nch):
                    if n - ci * 128 < 128:
                        continue
                    lc = ci * 128 + 127
                    self.cp(ebl[:, hd, ci:ci + 1], eb[:, lc:lc + 1], ["eb"], [("ebl", hd, ci)])
                    self.ts(KE[:, ci * 128:ci * 128 + 128], KI[:, ci * 128:ci * 128 + 128],
                            ebl[:, hd, ci:ci + 1], None, ALU.mult, ALU.bypass, [kKI, ("ebl", hd, ci)], [("kend", par, ci)])

            def st_gb(hd):
                par = hd % 2
                QD, KI, KE = qd[par], kinv[par], kend[par]
                kQD, kKI = "qd%d" % par, "kinv%d" % par
                ob, osq, kob = obs[par], osqs[0], "ob%d" % par
                sgb = sgbs[par]
                P.tag = ("B", bi, hd, "gb")
                for et in range(2):
                    wg, kg_ = self.wload(("ab", j, "gb", 2 * hd + et))
                    pgb, kpg = self.bank()
                    self.lin(pgb, kpg, 128, wg, kg_, 128, 0, self.hn, hnk(bi), b0, n)
                    self.act(sgb[:, et, 0:n], pgb[:, 0:n], AF.Silu, [kpg], [("sgb", par, et)])

            def st_chunk(hd, ci):
                par = hd % 2
                QD, KI, KE = qd[par], kinv[par], kend[par]
                kQD, kKI = "qd%d" % par, "kinv%d" % par
                ob, osq, kob = obs[par], osqs[0], "ob%d" % par
                sgb = sgbs[par]
                P.tag = ("B", bi, hd, "chunk")
                cn = min(128, n - ci * 128)
                cs = slice(ci * 128, ci * 128 + cn)
                first = (bi == 0 and ci == 0)
                last = (cn < 128)
                si = par
                pst, kst = self.bank()
                self.mm(pst[0:cn, 0:cn], KI[:, cs], QD[:, cs], True, True, [kKI, kQD], [kst])
                self.tt(stm[si][0:cn, 0:cn], pst[0:cn, 0:cn], self.cc(("tri",), 0, cn, slice(0, cn)), ALU.mult,
                        [kst, "C"], ["stm%d" % si])
                if not last:
                    ptr, ktr = self.bank()
                    ptb = ptr[:, 0:64].bitcast(BF16)
                    self.P.op("pe", (lambda o_, i_, id_: (lambda e: e.transpose(o_, i_, id_)))(ptb[:, 0:128], KE[:, cs], self.ident),
                              [("kend", par, ci), "cbf"], [ktr], cost=70.0)
                    self.act(ktk[si][:, 0:128], ptb[:, 0:128], AF.Copy, [ktr], ["ktk%d" % si])
                po, kpo = self.bank()
                for et in range(2):
                    vcol = hd * 256 + et * 128
                    self.mm(po[:, et * 128:et * 128 + cn], vtok[0:cn, ci, vcol:vcol + 128],
                            stm[si][0:cn, 0:cn], True, first, [("vtok", ci, 0), ("vtok", ci, 1), "stm%d" % si], [kpo])
                    if not first:
                        self.mm(po[:, et * 128:et * 128 + cn], Sb[:, hd, et * 128:(et + 1) * 128], QD[:, cs],
                                False, True, [("Sb", hd), kQD], [kpo])
                self.act(ob[:, 0:2, cs], po[:, 0:256].rearrange("p (e c) -> p e c", e=2)[:, :, 0:cn], AF.Copy,
                         [kpo], [kob])
                if not last:
                    pkv, kkv = self.bank()
                    self.mm(pkv[:, 0:256], ktk[si][:, 0:128], vtok[:, ci, hd * 256:(hd + 1) * 256], True, True,
                            ["ktk%d" % si, ("vtok", ci, 0), ("vtok", ci, 1)], [kkv])
                    self.stt(S[:, hd, :], S[:, hd, :], ebl[:, hd, ci:ci + 1], pkv[:, 0:256], ALU.mult, ALU.add,
                             [("S", hd), ("ebl", hd, ci), kkv], [("S", hd)])
                    self.act(Sb[:, hd, :], S[:, hd, :], AF.Copy, [("S", hd)], [("Sb", hd)])

            def st_post(hd):
                par = hd % 2
                QD, KI, KE = qd[par], kinv[par], kend[par]
                kQD, kKI = "qd%d" % par, "kinv%d" % par
                ob, osq, kob = obs[par], osqs[0], "ob%d" % par
                sgb = sgbs[par]
                if DBG.get("dumpB") and bi == DBG.get("dbi", 0) and hd == DBG.get("dhd", 0):
                    self.dump("ob0", ob[:, 0, :], [kob])
                    self.dump("ob1", ob[:, 1, :], [kob])
                    self.dump("qd", QD, [kQD])
                    self.dump("ki", KI, [kKI])
                    self.dump("ke", KE, [("kend", par, c_) for c_ in range(4)])
                    self.dump("S", S[:, hd, :], [("S", hd)])
                    self.dump("vt", vtok[:, 0, :], [("vtok", 0, 0), ("vtok", 0, 1)])
                    self.dump("ebl", ebl[:, hd, :], [("ebl", hd, c_) for c_ in range(4)])
                    self.dump("stm", stm[1], ["stm1"])
                    self.dump("ktk", ktk[1], ["ktk1"])
                P.tag = ("B", bi, hd, "post")
                pss, kss = self.bank()
                for et in range(2):
                    self.act(osq[:, et, 0:n], ob[:, et, 0:n], AF.Square, [kob], [("osq", et)])
                for et in range(2):
                    self.mm(pss[:, 0:n], self.ones, osq[:, et, 0:n], et == 0, et == 1, ["cbf1", ("osq", et)], [kss])
                self.act(t4[:, 0:n], pss[:, 0:n], AF.Sqrt, [kss, "epsc"], ["t4"], scale=1.0 / 256.0, bias=self.epsc)
                self.recip(t4[:, 0:n], t4[:, 0:n], ["t4"], ["t4"])
                for et in range(2):
                    g = 2 * hd + et
                    self.stt(ob[:, et, 0:n], ob[:, et, 0:n], self.cc(("gln", j), et), t4[:, 0:n], ALU.mult, ALU.mult,
                             [kob, "t4", "C"], [kob])
                    self.tt(self.mix[:, g, b0:b0 + n], ob[:, et, 0:n], sgb[:, et, 0:n], ALU.mult, [kob, ("sgb", par, et)], [("mix", g, bi)])
                    if DBG.get("dumpB") and bi == DBG.get("dbi", 0) and hd == DBG.get("dhd", 0):
                        self.dump("mix%d" % et, self.mix[:, g, b0:b0 + n], [("mix", g, bi)])
                        self.dump("sgb%d" % et, sgb[:, et, 0:n], [("sgb", par, et)])
                        if et == 1:
                            self.dump("rstd", t4[:, 0:n], ["t4"])


            for hp in range(2):
                hds = (2 * hp, 2 * hp + 1)
                for hd in hds:
                    st_pre(hd)
                for hd in hds:
                    st_gb(hd)
                for ci in range(nch):
                    for hd in hds:
                        st_chunk(hd, ci)
                for hd in hds:
                    st_post(hd)

        if DBG.get("dumpmix"):
            for g in range(8):
                self.dump("fm%d" % g, self.mix[:, g, 512:1024], [("mix", g, 1)])
        self.begin_layer(nxt_layer)
        if not DBG.get("skipB"):
            self.residual_bo(("ab", "woB"), j, lambda c, bi: [("mix", c, bi)], post=lambda bi: self.norm_for(nxt_layer, bi))

    def rope_tables(self):
        scr = self.mix2d.bitcast(F32)
        ang = scr[:, 0:T]
        kq = scr[:, T:2 * T]
        tab = scr[:, 2 * T:3 * T]
        pi_ = scr[:, 3 * T:3 * T + SEQ].bitcast(I32)
        self.P.dma("sp", pi_, self.posr[:, :], writes=["posi"])
        self.cp(ang[:, NMETA:T], pi_, ["posi"], ["ang"])
        self.ts(ang[:, NMETA:T], ang[:, NMETA:T], float(NMETA), None, ALU.add, ALU.bypass, ["ang"], ["ang"])
        self.ts(ang[:, NMETA:T], ang[:, NMETA:T], self.cc(("invf",)), None, ALU.mult, ALU.bypass, ["ang", "C"], ["ang"])
        self.ts(ang[:, 0:NMETA], self.cc(("metapos",), 0, 16), self.cc(("invf",)), None, ALU.mult, ALU.bypass, ["C"], ["angm"])
        MAGIC = 12582912.0
        C1 = 6.28125
        C2 = 2.0 * math.pi - 6.28125
        for dst, shift, key in ((self.St, 0.0, "St"), (self.Ct, math.pi / 2.0, "Ct")):
            if shift != 0.0:
                self.ts(ang, ang, shift, None, ALU.add, ALU.bypass, ["ang", "angm"], ["ang", "angm"])
            self.ts(kq, ang, 1.0 / (2.0 * math.pi), MAGIC, ALU.mult, ALU.add, ["ang", "angm"], ["kq"])
            self.ts(kq, kq, MAGIC, None, ALU.subtract, ALU.bypass, ["kq"], ["kq"])
            self.stt(tab, kq, -C1, ang, ALU.mult, ALU.add, ["kq", "ang", "angm"], ["tab"])
            self.stt(tab, kq, -C2, tab, ALU.mult, ALU.add, ["kq", "tab"], ["tab"])
            self.ts(tab, tab, 3.141592, -3.141592, ALU.min, ALU.max, ["tab"], ["tab"])
            self.act(tab, tab, AF.Sin, ["tab"], ["tab"])
            if key == "St":
                self.ts(dst, tab, self.cc(("sgn",)), None, ALU.mult, ALU.bypass, ["tab", "C"], [key])
            else:
                self.cp(dst, tab, ["tab"], [key])

    def rope(self, dst, pq, kpq, b0, n, t1, t2, wkeys, nope):
        Ct, St = self.Ct, self.St
        if nope:
            self.tt(dst[0:96, :], pq[0:96, 0:n], Ct[0:96, b0:b0 + n], ALU.mult, [kpq, "Ct"], [(wkeys[0], "n")])
            self.tt(t2[64:96, 0:n], pq[96:128, 0:n], St[96:128, b0:b0 + n], ALU.mult, [kpq, "St"], ["t2"])
            self.tt(dst[64:96, :], dst[64:96, :], t2[64:96, 0:n], ALU.add, ["t2", (wkeys[0], "n")], wkeys)
            return
        self.tt(t1[64:96, 0:n], pq[64:96, 0:n], Ct[64:96, b0:b0 + n], ALU.mult, [kpq, "Ct"], ["t1"])
        self.tt(t2[64:96, 0:n], pq[96:128, 0:n], St[96:128, b0:b0 + n], ALU.mult, [kpq, "St"], ["t2"])
        self.tt(dst[64:96, :], t1[64:96, 0:n], t2[64:96, 0:n], ALU.add, ["t1", "t2"], wkeys)

    def layer_c(self, j, nxt):
        P = self.P
        Ct, St = self.Ct, self.St
        hnk = lambda bi: [("hn", c, bi) for c in range(8)]
        P.fence()
        self.areset()
        cqn = self.abf(4 * T).rearrange("p (c t) -> p c t", c=4)
        ckvn = self.abf(2 * T).rearrange("p (c t) -> p c t", c=2)
        krt = self.abf(T)
        m0 = self.apos
        sq = self.abf(4 * 512).rearrange("p (c t) -> p c t", c=4)
        self.apos = m0
        pbuf = [self.abf(512) for _ in range(8)]
        rden = self.af32(512)
        t1 = self.af32(512)
        t2 = self.af32(512)
        t3 = self.af32(512)
        wkr, kkr = self.wload(("c", j, "kr"))
        for name, nt, dst, gk, D in (("cq", 4, cqn, ("qn", j), 512.0), ("ckv", 2, ckvn, ("kvn", j), 256.0)):
            wt = [self.wload(("c", j, name, t)) for t in range(nt)]
            for bi, (b0, n) in enumerate(BLKS):
                pbs = [self.bank() for _ in range(nt)]
                for t in range(nt):
                    self.lin(pbs[t][0], pbs[t][1], 128, wt[t][0], wt[t][1], 128, 0, self.hn, hnk(bi), b0, n)
                    self.act(sq[:, t, 0:n], pbs[t][0][:, 0:n], AF.Square, [pbs[t][1]], [("sq", t)])
                pss, kss = self.bank()
                for t in range(nt):
                    self.mm(pss[:, 0:n], self.ones, sq[:, t, 0:n], t == 0, t == nt - 1, ["cbf1", ("sq", t)], [kss])
                self.act(t1[:, 0:n], pss[:, 0:n], AF.Sqrt, [kss, "epsc"], ["t1"], scale=1.0 / D, bias=self.epsc)
                self.recip(t2[:, 0:n], t1[:, 0:n], ["t1"], ["t2"])
                for t in range(nt):
                    self.stt(dst[:, t, b0:b0 + n], pbs[t][0][:, 0:n], self.cc(gk, t), t2[:, 0:n], ALU.mult, ALU.mult,
                             [pbs[t][1], "t2", "C"], [(name, t, bi)])
        for bi, (b0, n) in enumerate(BLKS):
            pa, ka = self.bank()
            self.lin(pa, ka, 128, wkr, kkr, 128, 0, self.hn, hnk(bi), b0, n)
            self.rope(krt[:, b0:b0 + n], pa, ka, b0, n, t1, t2, [("krt", bi)], nope=False)
        for g in range(8):
            wg, kg_ = self.wload(("c", j, "gate", g))
            for bi, (b0, n) in enumerate(BLKS):
                pg, kpg = self.bank()
                self.lin(pg, kpg, 128, wg, kg_, 128, 0, self.hn, hnk(bi), b0, n)
                self.act(self.mix[:, g, b0:b0 + n], pg[:, 0:n], AF.Silu, [kpg], [("mix", g, bi)])
        P.fence()
        qh = [self.hn[:, 0, :], self.hn[:, 1, :]]
        kh = [self.hn[:, 2, :], self.hn[:, 3, :]]
        vh = [self.hn[:, 4:6, :].rearrange("p c t -> p (c t)")[:, 0:17 * 128].rearrange("p (t f) -> p t f", t=17),
              self.hn[:, 6:8, :].rearrange("p c t -> p (c t)")[:, 0:17 * 128].rearrange("p (t f) -> p t f", t=17)]
        for s in range(2):
            self.cp(kh[s][64:96, :], krt[64:96, :], [("krt", bi) for bi in range(5)], [("khr", s)])
            self.memset(vh[s][:, :, 64:128], 1.0, [("vh1", s)])
        cqk = lambda bi: [("cq", t, bi) for t in range(4)]
        ckk = lambda bi: [("ckv", t, bi) for t in range(2)]
        scale = 96.0 ** -0.5
        G_PO = ([6, 7], "po")
        G_ST = ([0, 1, 2, 3, 4, 5], "st")
        for hh in range(16):
            s = hh % 2
            wq, kq_ = self.wload(("c", j, "q", hh))
            if hh % 2 == 0:
                wkn, kkn = self.wload(("c", j, "kn2", hh // 2))
            wvv, kvv = self.wload(("c", j, "vv", hh))
            for bi, (b0, n) in enumerate(BLKS):
                pq, kpq = self.bank(G_ST)
                self.lin(pq, kpq, 128, wq, kq_, 128, 0, cqn, cqk(bi), b0, n, nk=4)
                self.rope(qh[s][:, b0:b0 + n], pq, kpq, b0, n, t1, t2, [("qh", s, bi)], nope=True)
                if hh % 2 == 0:
                    pkn, kpkn = self.bank(G_ST)
                    self.lin(pkn, kpkn, 128, wkn, kkn, 128, 0, ckvn, ckk(bi), b0, n, nk=2)
                    self.act(kh[0][0:64, b0:b0 + n], pkn[0:64, 0:n], AF.Copy, [kpkn], [("khn", 0, bi)])
                    self.act(kh[1][0:64, b0:b0 + n], pkn[64:128, 0:n], AF.Copy, [kpkn], [("khn", 1, bi)])
            for t0 in range(0, 17, 8):
                pv, kpv = self.bank(G_ST)
                tl = list(range(t0, min(17, t0 + 8)))
                for tt_ in tl:
                    tn = 128 if tt_ < 16 else 16
                    bi = min(tt_ // 4, 4)
                    for c in range(2):
                        self.mm(pv[0:tn, (tt_ - t0) * 64:(tt_ - t0) * 64 + 64], ckvn[:, c, tt_ * 128:tt_ * 128 + tn],
                                wvv[:, c * 64:(c + 1) * 64], c == 0, c == 1, [kvv, ("ckv", c, bi)], [kpv])
                if len(tl) == 8:
                    self.act(vh[s][:, t0:t0 + 8, 0:64], pv[:, 0:512].rearrange("p (t f) -> p t f", t=8), AF.Copy,
                             [kpv], [("vh", s, t0)])
                else:
                    self.act(vh[s][0:16, 16, 0:64], pv[0:16, 0:64], AF.Copy, [kpv], [("vh", s, t0)])
            vkeys = [("vh", s, 0), ("vh", s, 8), ("vh", s, 16), ("vh1", s)]
            for bi, (b0, n) in enumerate(BLKS):
                po, kpo = self.bank(G_PO)
                nkt = (b0 + n + 127) // 128
                kt_start = 0
                if n < 512:
                    pst, kst = self.bank(G_ST)
                    for kt in range(16):
                        self.mm(pst[:, kt * n:(kt + 1) * n], kh[s][0:96, kt * 128:kt * 128 + 128], qh[s][0:96, b0:b0 + n], True, True,
                                [("khr", s), ("khn", s, min(kt // 4, 4)), ("qh", s, bi), (("qh", s, bi), "n")], [kst])
                    pb_ = pbuf[7]
                    self.act(pb_[:, 0:16 * n], pst[:, 0:16 * n], AF.Exp, [kst], ["pb7"], scale=scale)
                    for kt in range(16):
                        self.mm(po[:, 0:n], vh[s][:, kt, :], pb_[:, kt * n:(kt + 1) * n], kt == 0, False, vkeys + ["pb7"], [kpo])
                    kt_start = 16
                for kt in range(kt_start, nkt):
                    kn_ = 128 if kt < 16 else 16
                    m = kt - b0 // 128
                    q0 = max(0, m) * 128 if n == 512 else 0
                    qn_ = n - q0
                    pst, kst = self.bank(G_ST)
                    diag = (m >= 0)
                    self.mm(pst[0:kn_, 0:qn_], kh[s][0:96, kt * 128:kt * 128 + kn_], qh[s][0:96, b0 + q0:b0 + n], True, not diag,
                            [("khr", s), ("khn", s, min(kt // 4, 4)), ("qh", s, bi), (("qh", s, bi), "n")], [kst])
                    if diag:
                        dn = min(128, qn_)
                        self.mm(pst[0:kn_, 0:dn], self.ident[0:kn_, 0:kn_], self.negtri[0:kn_, 0:dn], False, True, ["cbf"], [kst])
                    pi_ = kt % 8
                    pb_ = pbuf[pi_]
                    self.act(pb_[0:kn_, 0:qn_], pst[0:kn_, 0:qn_], AF.Exp, [kst], ["pb%d" % pi_], scale=scale)
                    self.mm(po[:, q0:n], vh[s][0:kn_, kt, :], pb_[0:kn_, 0:qn_], kt == 0, kt == nkt - 1,
                            vkeys + ["pb%d" % pi_], [kpo])
                self.recip(rden[64:128, 0:n], po[64:128, 0:n], [kpo], ["rden"])
                g = hh // 2
                r0 = (hh % 2) * 64
                self.tt(t3[r0:r0 + 64, 0:n], po[0:64, 0:n], rden[64:128, 0:n], ALU.mult, [kpo, "rden"], ["t3"])
                self.tt(self.mix[r0:r0 + 64, g, b0:b0 + n], t3[r0:r0 + 64, 0:n], self.mix[r0:r0 + 64, g, b0:b0 + n], ALU.mult,
                        ["t3", ("mix", g, bi)], [("mix", g, bi)])
        self.begin_layer(nxt)
        self.residual_bo(("c", "wo"), j, lambda c, bi: [("mix", c, bi)], post=lambda bi: self.norm_for(nxt, bi))

    def build(self):
        P = self.P
        self.prologue()
        self.areset()
        cs = self.af32(2)
        self.epsc = cs[:, 0:1]
        self.onec = cs[:, 1:2]
        self.memset(self.epsc, EPS, ["epsc"])
        self.memset(self.onec, 1.0, ["onec"])
        if self.nlayers > 1:
            self.rope_tables()
        self.abase = self.apos
        self.areset = lambda: setattr(self, "apos", self.abase)
        kinds = [("ab", l // 2) if l % 2 == 0 else ("c", l // 2) for l in range(self.nlayers)]
        self.okeys = []
        self.otc = 0
        self.begin_layer(kinds[0])
        for bi in range(len(BLKS)):
            self.norm_for(kinds[0], bi)
        for li, kind in enumerate(kinds):
            nxt = kinds[li + 1] if li + 1 < len(kinds) else None
            if kind[0] == "ab":
                self.layer_ab(kind[1], nxt)
            else:
                self.layer_c(kind[1], nxt)
        okeys = self.okeys
        if not self.final:
            for c in range(8):
                for bi, (b0, n) in enumerate(BLKS):
                    k = ("out", c, bi)
                    P.dma("sp", self.outT[c * 128:(c + 1) * 128, b0:b0 + n], self.h[:, c, b0:b0 + n],
                          reads=self.hkeys(c, bi), writes=[k])
                    okeys.append(k)
        P.wait_for("sp", okeys + self.dumpkeys)
        P.build()
        return self.nc


_CACHE = {}


def host_inputs(inputs):
    wl = weight_tiles(inputs)
    wts = np.ascontiguousarray(np.concatenate([a.astype(np.float32) for _, _, a in wl], axis=1))
    cl = const_cols(inputs)
    cst = np.ascontiguousarray(np.concatenate([a.astype(np.float32) for _, _, a in cl], axis=1))
    x = np.asarray(inputs["x"], np.float32)
    pos = np.asarray(inputs["positions"], np.int32)
    metaT = np.ascontiguousarray(np.asarray(inputs["meta_tokens"], np.float32).T)
    maps = []
    for b in range(x.shape[0]):
        maps.append({
            "xT": np.ascontiguousarray(x[b].T),
            "metaT": metaT,
            "posr": np.ascontiguousarray(np.broadcast_to(pos[b][None, :], (128, SEQ))),
            "wts": wts,
            "cst": cst,
        })
    return maps


def kernel(**inputs):
    inputs = {k: np.asarray(v) for k, v in inputs.items()}
    if "nc" not in _CACHE:
        _CACHE["nc"] = Builder(4, True).build()
    nc = _CACHE["nc"]
    maps = host_inputs(inputs)
    res = run_bass_kernel_spmd(nc, maps, core_ids=list(range(8)))
    out = np.stack([np.ascontiguousarray(r["outT"].T) for r in res.results], axis=0)
    return out.astype(np.float32)
```
